# Optimizing a Trainium2 kernel written in Bass

```python
import math
import jax, jax.numpy as jnp
from jax import lax
import numpy as np

D_MODEL = 2048
BATCH = 1
SEQ = 8192
DEPTH = 2
DEC_BATCH = 16
DEC_SEQ = 64
PAST_LEN = 4096

CHUNK = 64
N_LEFT_CHUNKS = 8
LEFT_ROWS = N_LEFT_CHUNKS * CHUNK
BAND = (N_LEFT_CHUNKS + 1) * CHUNK
N_HEADS = 16
HEAD_DIM = D_MODEL // N_HEADS
MAX_REL = 256
N_REL = 2 * MAX_REL + 1
CONV_W = 3
D_FF = -(-8 * D_MODEL // (3 * 256)) * 256
N_A = DEPTH // 2
N_B = DEPTH - N_A
ALPHA = (2.0 * DEPTH) ** 0.25
BETA = (8.0 * DEPTH) ** -0.25
LN_EPS = 1e-5

kernel_name = "yoco_shortconv_chunkband_deepnorm_step"


def layer_norm(x, g, b):
    xf = x.astype(jnp.float32)
    mu = jnp.mean(xf, axis=-1, keepdims=True)
    var = jnp.mean(jnp.square(xf - mu), axis=-1, keepdims=True)
    return ((xf - mu) * lax.rsqrt(var + LN_EPS) * g + b).astype(x.dtype)


def post_norm(x, sub, g, b):
    return layer_norm(ALPHA * x + sub, g, b)


def swiglu(x, w_gate_up, w_down):
    g, u = jnp.split(x @ w_gate_up, 2, axis=-1)
    return (jax.nn.silu(g) * u) @ w_down


def short_conv_mixer(x, hist, w_in, conv_w, w_out):
    s = x.shape[1]
    b, c, h = jnp.split(x @ w_in, 3, axis=-1)
    u = c * h
    full = jnp.concatenate([hist.astype(u.dtype), u], axis=1)
    conv = conv_w[0] * full[:, 0:s]
    for t in range(1, CONV_W):
        conv = conv + conv_w[t] * full[:, t:t + s]
    return (b * conv) @ w_out, full[:, -(CONV_W - 1):]


def rel_bias_lookup(rel_bias, rel):
    return rel_bias[:, jnp.clip(rel, -MAX_REL, MAX_REL) + MAX_REL]


def band_attention_prompt(q, k, v, rel_bias):
    n, s = q.shape[:2]
    nc = s // CHUNK
    pad = jnp.zeros((n, LEFT_ROWS, N_HEADS, HEAD_DIM), k.dtype)
    kp = jnp.concatenate([pad, k], axis=1)
    vp = jnp.concatenate([pad.astype(v.dtype), v], axis=1)
    idx = jnp.arange(nc)[:, None] * CHUNK + jnp.arange(BAND)[None, :]
    kb = kp[:, idx]
    vb = vp[:, idx]
    qc = q.reshape(n, nc, CHUNK, N_HEADS, HEAD_DIM)
    rel = jnp.arange(CHUNK)[:, None] - jnp.arange(BAND)[None, :] + LEFT_ROWS
    bias = rel_bias_lookup(rel_bias, rel).astype(jnp.float32)
    valid = (idx - LEFT_ROWS) >= 0
    scores = jnp.einsum('ncqhd,nckhd->nchqk', qc, kb).astype(jnp.float32) * (HEAD_DIM ** -0.5)
    scores = jnp.where(valid[None, :, None, None, :], scores + bias[None, None], -jnp.inf)
    p = jax.nn.softmax(scores, axis=-1).astype(v.dtype)
    out = jnp.einsum('nchqk,nckhd->ncqhd', p, vb)
    return out.reshape(n, s, N_HEADS * HEAD_DIM)


def attention_with_past(q, k_all, v_all, rel_bias, n_past):
    n, s = q.shape[:2]
    rel = (n_past + jnp.arange(s))[:, None] - jnp.arange(n_past + s)[None, :]
    bias = rel_bias_lookup(rel_bias, rel).astype(jnp.float32)
    scores = jnp.einsum('nqhd,nkhd->nhqk', q, k_all).astype(jnp.float32) * (HEAD_DIM ** -0.5)
    p = jax.nn.softmax(scores + bias[None], axis=-1).astype(v_all.dtype)
    out = jnp.einsum('nhqk,nkhd->nqhd', p, v_all)
    return out.reshape(n, s, N_HEADS * HEAD_DIM)


def run_trunk(x, conv_hist, cache_k, cache_v, w_in_a, conv_w, w_out_a, w_kv, w_q, w_o,
              rel_bias, ln_g, ln_b, w_gate_up, w_down):
    n, s, _ = x.shape
    new_conv = []
    k = v = k_all = v_all = None
    for l in range(DEPTH):
        if l < N_A:
            mix, st = short_conv_mixer(x, conv_hist[l], w_in_a[l], conv_w[l], w_out_a[l])
            new_conv.append(st)
        else:
            if l == N_A:
                k, v = jnp.split(x @ w_kv, 2, axis=-1)
                k = k.reshape(n, s, N_HEADS, HEAD_DIM)
                v = v.reshape(n, s, N_HEADS, HEAD_DIM)
                if cache_k is not None:
                    k_all = jnp.concatenate([cache_k.astype(k.dtype), k], axis=1)
                    v_all = jnp.concatenate([cache_v.astype(v.dtype), v], axis=1)
            bl = l - N_A
            q = (x @ w_q[bl]).reshape(n, s, N_HEADS, HEAD_DIM)
            if cache_k is None:
                att = band_attention_prompt(q, k, v, rel_bias[bl])
            else:
                att = attention_with_past(q, k_all, v_all, rel_bias[bl], cache_k.shape[1])
            mix = att @ w_o[bl]
        x = post_norm(x, mix, ln_g[l, 0], ln_b[l, 0])
        x = post_norm(x, swiglu(x, w_gate_up[l], w_down[l]), ln_g[l, 1], ln_b[l, 1])
    return x, jnp.stack(new_conv), k, v


def setup_inputs(seed: int = 0) -> dict:
    key = jax.random.key(seed)
    ks = jax.random.split(key, 20)

    def nrm(k, shape, scale):
        return jax.random.normal(k, shape, jnp.float32) * scale

    r = min(LEFT_ROWS, PAST_LEN)
    d = D_MODEL
    x_prompt = nrm(ks[0], (BATCH, SEQ, d), 1.0)
    x_sample = nrm(ks[1], (DEC_BATCH, DEC_SEQ, d), 1.0)
    state_conv = nrm(ks[2], (N_A, DEC_BATCH, CONV_W - 1, d), 1.0)
    cache_k = nrm(ks[3], (DEC_BATCH, r, N_HEADS, HEAD_DIM), 1.0)
    cache_v = nrm(ks[4], (DEC_BATCH, r, N_HEADS, HEAD_DIM), 1.0)
    w_in_a = nrm(ks[5], (N_A, d, 3 * d), d ** -0.5)
    conv_w = nrm(ks[6], (N_A, CONV_W, d), CONV_W ** -0.5)
    w_out_a = nrm(ks[7], (N_A, d, d), BETA * d ** -0.5)
    w_kv = jnp.concatenate([nrm(ks[8], (d, d), d ** -0.5),
                            nrm(ks[9], (d, d), BETA * d ** -0.5)], axis=1)
    w_q = nrm(ks[10], (N_B, d, d), d ** -0.5)
    w_o = nrm(ks[11], (N_B, d, d), BETA * d ** -0.5)
    rel_bias = nrm(ks[12], (N_B, N_HEADS, N_REL), 0.1)
    ln_g = 1.0 + nrm(ks[13], (DEPTH, 2, d), 0.05)
    ln_b = nrm(ks[14], (DEPTH, 2, d), 0.05)
    w_gate_up = nrm(ks[15], (DEPTH, d, 2 * D_FF), d ** -0.5)
    w_down = nrm(ks[16], (DEPTH, D_FF, d), BETA * D_FF ** -0.5)
    return {"x_prompt": x_prompt, "x_sample": x_sample, "state_conv": state_conv,
            "cache_k": cache_k, "cache_v": cache_v, "w_in_a": w_in_a, "conv_w": conv_w,
            "w_out_a": w_out_a, "w_kv": w_kv, "w_q": w_q, "w_o": w_o, "rel_bias": rel_bias,
            "ln_g": ln_g, "ln_b": ln_b, "w_gate_up": w_gate_up, "w_down": w_down}


def reference(x_prompt, x_sample, state_conv, cache_k, cache_v, w_in_a, conv_w, w_out_a,
              w_kv, w_q, w_o, rel_bias, ln_g, ln_b, w_gate_up, w_down):
    weights = (w_in_a, conv_w, w_out_a, w_kv, w_q, w_o, rel_bias, ln_g, ln_b, w_gate_up, w_down)
    conv_zero = jnp.zeros((N_A, x_prompt.shape[0], CONV_W - 1, D_MODEL), x_prompt.dtype)
    y_prompt, conv_prompt, k_full, v_full = run_trunk(x_prompt, conv_zero, None, None, *weights)
    y_sample, conv_sample, k_sample, v_sample = run_trunk(x_sample, state_conv, cache_k, cache_v, *weights)
    k_prompt = k_full[:, -LEFT_ROWS:]
    v_prompt = v_full[:, -LEFT_ROWS:]
    return (y_prompt, y_sample, conv_prompt, conv_sample, k_prompt, v_prompt, k_sample, v_sample)
```

```python
import contextlib
import numpy as np
import concourse.bass as bass
import concourse.mybir as mybir
from concourse.bass_utils import run_bass_kernel_spmd

F32 = mybir.dt.float32
BF16 = mybir.dt.bfloat16
AF = mybir.ActivationFunctionType
ALU = mybir.AluOpType

D = 2048
KC = 16
DFF = 5632
FC = 44
FH = 22
NH = 16
ALPHA = 4.0 ** 0.25
EPS = 1e-5
QSCALE = 128.0 ** -0.5
NCORES = 8
GW = 576
NSLOT = 4
NSTG = 3
ARC = 4736
SMALLW = False
STOP = None


class _Stop(Exception):
    pass


def _chk(tag):
    if STOP == tag:
        raise _Stop()


class Sched:
    def __init__(self, nc, sems, dma_channels):
        self.nc = nc
        self.eng = {"pe": nc.tensor, "dve": nc.vector, "act": nc.scalar,
                    "pool": nc.gpsimd, "sp": nc.sync}
        self.sem = dict(sems)
        self.n = {k: 0 for k in self.sem}
        self.sigs = {k: [] for k in self.sem}
        self.waited = {}
        self.last_w = {}
        self.readers = {}
        self.dma_channels = set(dma_channels)
        self.n_wait = 0
        self.n_ins = 0

    def _sem_value(self, e, idx):
        if e in self.dma_channels:
            return 16 * (idx + 1)
        s = self.sigs[e]
        lo, hi = 0, len(s)
        while lo < hi:
            mid = (lo + hi) // 2
            if s[mid] >= idx:
                hi = mid
            else:
                lo = mid + 1
        assert lo < len(s), f"no signalled instr on {e} at/after {idx}"
        return lo + 1

    def _deps(self, reads, writes):
        deps = []
        lw = self.last_w
        for k in reads:
            w = lw.get(k)
            if w is not None:
                deps.append(w)
        for k in writes:
            w = lw.get(k)
            if w is not None:
                deps.append(w)
            r = self.readers.get(k)
            if r:
                deps.extend(r.items())
        return deps

    def _emit_waits(self, on, deps, skip_same=False):
        best = {}
        for e, i in deps:
            if skip_same and e == on:
                continue
            if e not in best or i > best[e]:
                best[e] = i
        for e, i in best.items():
            v = self._sem_value(e, i)
            if self.waited.get((on, e), 0) >= v:
                continue
            self.eng[on].wait_ge(self.sem[e], v)
            self.waited[(on, e)] = v
            self.n_wait += 1

    def _record(self, who, idx, reads, writes):
        for k in reads:
            self.readers.setdefault(k, {})[who] = idx
        for k in writes:
            self.last_w[k] = (who, idx)
            self.readers[k] = {}

    def op(self, on, ins_fn, reads=(), writes=(), signal=True):
        ex = [k for k in reads if k[:3] in ("mm:", "tp:")]
        if ex and on != "pe":
            writes = list(writes) + ex
        self._emit_waits(on, self._deps(reads, writes), skip_same=(on == "pe"))
        ins = ins_fn()
        idx = self.n[on]
        self.n[on] += 1
        self.n_ins += 1
        if signal:
            ins.then_inc(self.sem[on], 1)
            self.sigs[on].append(idx)
        self._record(on, idx, reads, writes)
        return ins

    def dma(self, queue, chan, out, in_, reads=(), writes=()):
        self._emit_waits(queue, self._deps(reads, writes))
        ins = self.eng[queue].dma_start(out=out, in_=in_)
        ins.then_inc(self.sem[chan], 16)
        idx = self.n[chan]
        self.n[chan] += 1
        self.n_ins += 1
        self._record(chan, idx, reads, writes)
        return ins

    def wait_all(self, on, keys):
        self._emit_waits(on, [self.last_w[k] for k in keys if k in self.last_w])


def build_program():
    nc = bass.Bass("TRN2", target_bir_lowering=False)

    def din(name, shape):
        return nc.dram_tensor(name, list(shape), F32, kind="ExternalInput")

    def dout(name, shape):
        return nc.dram_tensor(name, list(shape), F32, kind="ExternalOutput")

    xp = din("xp", [1538, D])
    xs = din("xs", [128, D])
    sconv = din("sconv", [2, 32, 128])
    ck = din("ck", [2, 512, D])
    cv = din("cv", [2, 512, D])
    hv = din("hv", [128, 1])
    if SMALLW:
        _din = din
        din = lambda name, shape: _din(name, [128, 128] if name.startswith("w_") else shape)
    w_in = din("w_in", [D, 3 * D])
    conv_w = din("conv_w", [48, 128])
    w_out = din("w_out", [D, D])
    w_kv = din("w_kv", [D, 2 * D])
    w_q = din("w_q", [D, D])
    w_o = din("w_o", [D, D])
    rel_bias = din("rel_bias", [NH, 513])
    ln_g = din("ln_g", [64, 128])
    ln_b = din("ln_b", [64, 128])
    w_gu = [din("w_gu0", [D, 2 * DFF]), din("w_gu1", [D, 2 * DFF])]
    w_dn = [din("w_dn0", [DFF, D]), din("w_dn1", [DFF, D])]

    y_p = dout("y_p", [1024, D])
    y_s = dout("y_s", [128, D])
    conv_p = dout("conv_p", [32, 128])
    conv_s = dout("conv_s", [2, 32, 128])
    k_p = dout("k_p", [512, D])
    v_p = dout("v_p", [512, D])
    k_s = dout("k_s", [128, D])
    v_s = dout("v_s", [128, D])
    ebias = nc.dram_tensor("ebias", [NH, 768], F32)

    dma_chans = ([f"w{i}" for i in range(NSLOT)] + [f"g{i}" for i in range(NSTG)]
                 + ["hk0", "hk1", "ks0", "ks1", "vs0", "vs1", "hv", "rb", "eb"])
    sem_names = ["pe", "dve", "act", "pool", "sp"] + dma_chans
    with contextlib.ExitStack() as st:
        sems = {n: st.enter_context(nc.semaphore(n)) for n in sem_names}
        S = Sched(nc, sems, dma_chans)

        def sb(name, shape, dt):
            return st.enter_context(nc.sbuf_tensor(name, list(shape), dt))

        def ps(name, shape, dt=F32):
            return st.enter_context(nc.psum_tensor(name, list(shape), dt))

        x32T = sb("x32T", [128, KC, GW], F32)
        xT = sb("xT", [128, KC, GW], BF16)
        KT = sb("KT", [128, NH, 1024], BF16)
        KTs = sb("KTs", [128, NH, 64], BF16)
        V = sb("V", [128, 8, D], BF16)
        Vs = sb("Vs", [128, D], BF16)
        R = sb("R", [128, FH, GW], BF16)
        WR = [sb(f"WR{i}", [128, FH, 128], BF16) for i in range(NSLOT)]
        STG = [sb(f"STG{i}", [128, 512], F32) for i in range(NSTG)]
        AR = sb("AR", [128, ARC], F32)
        PT = [sb(f"PT{i}", [128, 640], BF16) for i in range(2)]
        KTc = [sb(f"KTc{i}", [128, 512], BF16) for i in range(2)]
        Vc = [sb(f"Vc{i}", [128, 4, 128], BF16) for i in range(2)]
        ident = sb("ident", [128, 128], F32)
        ones_f = sb("ones_f", [128, 128], F32)
        ones_b = sb("ones_b", [128, 128], BF16)
        ones_hv = sb("ones_hv", [128, 128], BF16)
        sel = sb("sel", [2, 2, 128], F32)
        stats = sb("stats", [128, 5, 4, 6], F32)
        mv = sb("mv", [128, 5, 2], F32)
        rs2 = sb("rs2", [128, 5, 2], F32)
        std = sb("std", [128, 5], F32)
        gcol = sb("gcol", [128, 64], F32)
        bcol = sb("bcol", [128, 64], F32)
        cw = sb("cw", [128, 48], F32)
        carry = sb("carry", [128, 2, KC], F32)
        shT = sb("shT", [128, 2, 32], F32)
        sstate = sb("sstate", [128, 2, KC], F32)
        hvt = sb("hvt", [128, 1], F32)

        MM = [ps(f"MM{i}", [128, 1024]) for i in range(3)]
        TP = [ps(f"TP{i}", [128, 512]) for i in range(2)]

        def ar(c0, n):
            return AR[:, c0:c0 + n]

        def ark(c0, n):
            return [f"ar:{p}" for p in range(c0 // 64, (c0 + n - 1) // 64 + 1)]

        cnt = {"mm": 0, "tp": 0, "stg": 0, "w": 0, "ev": 0}

        def next_mm():
            s = cnt["mm"] % 3
            cnt["mm"] += 1
            return s

        def next_tp():
            s = cnt["tp"] % 2
            cnt["tp"] += 1
            return s

        def next_stg():
            s = cnt["stg"] % NSTG
            cnt["stg"] += 1
            return s

        def ev_eng():
            cnt["ev"] += 1
            return "act" if cnt["ev"] % 2 else "dve"

        def copy_on(eng, out, in_):
            if eng == "act":
                return nc.scalar.copy(out, in_)
            if eng == "dve":
                return nc.vector.tensor_copy(out, in_)
            return nc.gpsimd.tensor_copy(out, in_)

        if STOP == "empty":
            return nc
        S.op("pool", lambda: nc.gpsimd.memset(ones_f[:], 1.0), writes=["ones_f"])
        S.op("pool", lambda: nc.gpsimd.memset(ones_b[:], 1.0), writes=["ones_b"])
        S.op("pool", lambda: nc.gpsimd.affine_select(
            out=ident[:], in_=ones_f[:], pattern=[[1, 128]], compare_op=ALU.is_equal,
            fill=0.0, base=0, channel_multiplier=-1), reads=["ones_f"], writes=["ident"])
        S.op("pool", lambda: nc.gpsimd.affine_select(
            out=sel[:, 0, :], in_=ones_f[0:2, :], pattern=[[0, 128]], compare_op=ALU.is_equal,
            fill=0.0, base=0, channel_multiplier=1), reads=["ones_f"], writes=["sel"])
        S.op("pool", lambda: nc.gpsimd.affine_select(
            out=sel[:, 1, :], in_=ones_f[0:2, :], pattern=[[0, 128]], compare_op=ALU.is_equal,
            fill=0.0, base=-1, channel_multiplier=1), reads=["ones_f"], writes=["sel"])
        S.dma("sp", "hv", hvt[:], hv[:], writes=["hvt"])
        S.op("act", lambda: nc.scalar.activation(out=ones_hv[:], in_=ones_f[:], func=AF.Identity,
                                                 scale=hvt[:]), reads=["ones_f", "hvt"], writes=["ones_hv"])

        def load_cols(src, nrows, dst, key):
            sg = next_stg()
            S.dma("sp", f"g{sg}", STG[sg][0:nrows, 0:128], src, writes=[f"stg:{sg}"])
            t = next_tp()
            S.op("pe", lambda: nc.tensor.transpose(TP[t][:, 0:nrows], STG[sg][0:nrows, 0:128], ident[0:nrows, 0:nrows]),
                 reads=[f"stg:{sg}", "ident"], writes=[f"tp:{t}"])
            S.op("dve", lambda: nc.vector.tensor_copy(dst, TP[t][:, 0:nrows]), reads=[f"tp:{t}"], writes=[key])

        load_cols(ln_g[:], 64, gcol[:], "gcol")
        load_cols(ln_b[:], 64, bcol[:], "bcol")
        load_cols(conv_w[:], 48, cw[:], "cw")
        for s_ in range(2):
            load_cols(sconv[s_], 32, shT[:, s_, :], f"shT:{s_}")
        S.op("dve", lambda: nc.vector.memset(carry[:], 0.0), writes=["carry"])

        rb = AR[0:NH, 0:513]
        vv = AR[0:NH, 576:576 + 768]
        S.dma("sp", "rb", rb, rel_bias[:], writes=ark(0, 513))
        S.op("act", lambda: nc.scalar.activation(out=rb, in_=rb, func=AF.Exp),
             reads=ark(0, 513), writes=ark(0, 513))
        S.op("dve", lambda: nc.vector.tensor_copy(AR[0:NH, 576:576 + 384], AR[0:NH, 512:513].to_broadcast([NH, 384])),
             reads=ark(0, 513), writes=ark(576, 384))
        S.op("dve", lambda: nc.vector.tensor_copy(AR[0:NH, 576 + 384:576 + 768],
                                                  bass.AP(AR, 511, [[ARC, NH], [-1, 384]])),
             reads=ark(0, 513), writes=ark(960, 384))
        S.dma("sp", "eb", ebias[:], vv, reads=ark(576, 768), writes=["ebias"])

        def wload(src, nk):
            slot = cnt["w"] % NSLOT
            cnt["w"] += 1
            S.dma("pool", f"w{slot}", WR[slot][:, 0:nk, :], src, writes=[f"wr:{slot}"])
            return slot

        def wview(wt, nk_total):
            return wt[:].rearrange("(k p) n -> p k n", p=128)

        WV = {
            "w_in": wview(w_in, KC), "w_out": wview(w_out, KC), "w_kv": wview(w_kv, KC),
            "w_q": wview(w_q, KC), "w_o": wview(w_o, KC),
            "w_gu0": wview(w_gu[0], KC), "w_gu1": wview(w_gu[1], KC),
            "w_dn0": wview(w_dn[0], FC), "w_dn1": wview(w_dn[1], FC),
        }

        class Seg:
            def __init__(self, c0, n):
                self.c0, self.n = c0, n

        def tl(c0, n):
            return range(c0 // 128, min(4, (c0 + n - 1) // 128) + 1)

        def x32k(f, c0, n):
            return [f"x32:{f}:{t}" for t in tl(c0, n)]

        def xTk(f, c0, n):
            return [f"xT:{f}:{t}" for t in tl(c0, n)]

        def rk(j, sg):
            return [f"R:{j}:{t}" for t in tl(sg.c0, sg.n)]

        def pieces(sg, g):
            if g == 0:
                return [(sg.c0, sg.n, "p")]
            out = []
            if sg.c0 < 512:
                out.append((sg.c0, min(sg.c0 + sg.n, 512) - sg.c0, "p"))
            if sg.c0 + sg.n > 512:
                c = max(sg.c0, 512)
                out.append((c, sg.c0 + sg.n - c, "s"))
            return out

        def proj(wname, col0, k0, nk, rhs_fn, rhs_keys_fn, segs):
            slot = wload(WV[wname][:, k0:k0 + nk, col0:col0 + 128], nk)
            m = next_mm()
            for k in range(nk):
                for si, sg in enumerate(segs):
                    last = (k == nk - 1) and (si == len(segs) - 1)
                    S.op("pe", lambda: nc.tensor.matmul(
                        MM[m][:, si * 512: si * 512 + sg.n], WR[slot][:, k, :], rhs_fn(k, sg),
                        start=(k == 0), stop=(k == nk - 1)),
                        reads=[f"wr:{slot}"] + rhs_keys_fn(k, sg), writes=[f"mm:{m}:{si}"], signal=last)
            return m

        def mmv(m, si, sg):
            return MM[m][:, si * 512: si * 512 + sg.n]

        def mmp(m, si, sg, pc0, pn):
            o = si * 512 + (pc0 - sg.c0)
            return MM[m][:, o:o + pn]

        def x_rhs(k, sg):
            return xT[:, k, sg.c0:sg.c0 + sg.n]

        def x_keys(k, sg):
            return xTk(k, sg.c0, sg.n)

        def r_rhs(k, sg):
            return R[:, k, sg.c0:sg.c0 + sg.n]

        def r_keys(k, sg):
            return rk(k, sg)

        def load_x(src, row0, r, c0):
            for q in range(4):
                sg = next_stg()
                S.dma("sp", f"g{sg}", STG[sg][0:r, :], src[row0:row0 + r, q * 512:(q + 1) * 512], writes=[f"stg:{sg}"])
                t = next_tp()
                for i in range(4):
                    S.op("pe", lambda: nc.tensor.transpose(TP[t][:, i * 128:i * 128 + r],
                                                           STG[sg][0:r, i * 128:(i + 1) * 128], ident[0:r, 0:r]),
                         reads=[f"stg:{sg}", "ident"], writes=[f"tp:{t}"], signal=(i == 3))
                tpv = TP[t][:].rearrange("p (i c) -> p i c", i=4)[:, :, 0:r]
                S.op("dve", lambda: nc.vector.tensor_copy(x32T[:, 4 * q:4 * q + 4, c0:c0 + r], tpv),
                     reads=[f"tp:{t}"], writes=[k_ for i in range(4) for k_ in x32k(4 * q + i, c0, r)])
                S.op("act", lambda: nc.scalar.copy(xT[:, 4 * q:4 * q + 4, c0:c0 + r], x32T[:, 4 * q:4 * q + 4, c0:c0 + r]),
                     reads=[k_ for i in range(4) for k_ in x32k(4 * q + i, c0, r)],
                     writes=[k_ for i in range(4) for k_ in xTk(4 * q + i, c0, r)])

        def store_tiles(tl_list):
            for q in range(4):
                for (dst, row0, r, c0) in tl_list:
                    t = next_tp()
                    for i in range(4):
                        S.op("pe", lambda: nc.tensor.transpose(TP[t][0:r, i * 128:(i + 1) * 128],
                                                               x32T[:, 4 * q + i, c0:c0 + r], ident[:, :]),
                             reads=x32k(4 * q + i, c0, r) + ["ident"], writes=[f"tp:{t}"], signal=(i == 3))
                    sg = next_stg()
                    e = ev_eng()
                    S.op(e, lambda: copy_on(e, STG[sg][0:r, :], TP[t][0:r, :]), reads=[f"tp:{t}"], writes=[f"stg:{sg}"])
                    S.dma("sp", f"g{sg}", dst[row0:row0 + r, q * 512:(q + 1) * 512], STG[sg][0:r, :],
                          reads=[f"stg:{sg}"], writes=[f"out:{dst.name}"])

        def ln_quarter(q, tiles):
            for ti, (c0, r) in enumerate(tiles):
                t = next_tp()
                for i in range(4):
                    S.op("pe", lambda: nc.tensor.transpose(TP[t][0:r, i * 128:(i + 1) * 128],
                                                           x32T[:, 4 * q + i, c0:c0 + r], ident[:, :]),
                         reads=x32k(4 * q + i, c0, r) + ["ident"], writes=[f"tp:{t}"], signal=(i == 3))
                S.op("dve", lambda: nc.vector.bn_stats(stats[0:r, ti, q, :], TP[t][0:r, :]),
                     reads=[f"tp:{t}"], writes=[f"stats:{ti}"])

        def proj_resid_ln(wname, rhs_fn, rhs_keys_fn, segs, tiles, k0=0, nk=KC, first=True, stats=True):
            for n in range(KC):
                m = proj(wname, n * 128, k0, nk, rhs_fn, rhs_keys_fn, segs)
                if stats and n % 4 == 0 and n > 0:
                    ln_quarter(n // 4 - 1, tiles)
                resid_evac(m, n, segs, first=first)
            if stats:
                ln_quarter(3, tiles)

        def layer_norm(ln_idx, segs, tiles, write_xT):
            tmpc = [0, 576]
            rowv_c = 1152
            rowv = AR[0:2, rowv_c:rowv_c + GW]

            def stats_a(ti):
                c0, r = tiles[ti]
                S.op("dve", lambda: nc.vector.bn_aggr(mv[0:r, ti, :], stats[0:r, ti, :, :].rearrange("p a b -> p (a b)")),
                     reads=[f"stats:{ti}"], writes=[f"mv:{ti}"])
                S.op("act", lambda: nc.scalar.activation(out=std[0:r, ti:ti + 1], in_=mv[0:r, ti, 1:2], func=AF.Sqrt,
                                                         bias=EPS, scale=1.0), reads=[f"mv:{ti}"], writes=[f"std:{ti}"])

            def stats_b(ti):
                c0, r = tiles[ti]
                S.op("dve", lambda: nc.vector.reciprocal(rs2[0:r, ti, 0:1], std[0:r, ti:ti + 1]),
                     reads=[f"std:{ti}"], writes=[f"rs2:{ti}"])
                S.op("dve", lambda: nc.vector.tensor_scalar(out=rs2[0:r, ti, 1:2], in0=mv[0:r, ti, 0:1], scalar1=rs2[0:r, ti, 0:1],
                                                            scalar2=-1.0, op0=ALU.mult, op1=ALU.mult),
                     reads=[f"mv:{ti}", f"rs2:{ti}"], writes=[f"rs2:{ti}"])
                t = next_tp()
                S.op("pe", lambda: nc.tensor.transpose(TP[t][0:2, 0:r], rs2[0:r, ti, 0:2], ident[0:r, 0:r]),
                     reads=[f"rs2:{ti}", "ident"], writes=[f"tp:{t}"])
                S.op("act", lambda: nc.scalar.copy(AR[0:2, rowv_c + c0:rowv_c + c0 + r], TP[t][0:2, 0:r]),
                     reads=[f"tp:{t}"], writes=ark(rowv_c + c0, r))

            stats_a(0)
            for ti in range(len(tiles)):
                if ti + 1 < len(tiles):
                    stats_a(ti + 1)
                stats_b(ti)
            ma, mb = next_mm(), next_mm()
            for si, sg in enumerate(segs):
                S.op("pe", lambda: nc.tensor.matmul(mmv(ma, si, sg), sel[:, 0, :], rowv[:, sg.c0:sg.c0 + sg.n],
                                                    start=True, stop=True),
                     reads=["sel"] + ark(rowv_c, GW), writes=[f"mm:{ma}:{si}"])
                S.op("pe", lambda: nc.tensor.matmul(mmv(mb, si, sg), sel[:, 1, :], rowv[:, sg.c0:sg.c0 + sg.n],
                                                    start=True, stop=True),
                     reads=["sel"] + ark(rowv_c, GW), writes=[f"mm:{mb}:{si}"])
            gi = ln_idx * KC
            merged = (len(segs) == 2 and segs[0].n == segs[1].n and segs[0].c0 + segs[0].n == segs[1].c0)
            for n in range(KC):
                tc = tmpc[n % 2]
                gsc, bsc = gcol[:, gi + n:gi + n + 1], bcol[:, gi + n:gi + n + 1]
                if merged:
                    c0, w, ns = segs[0].c0, segs[0].n, 2 * segs[0].n
                    tmp = AR[:, tc + c0:tc + c0 + ns]
                    zt = x32T[:, n, c0:c0 + ns]
                    tk = ark(tc + c0, ns)
                    v3 = lambda ap: ap.rearrange("p (s c) -> p s c", s=2)
                    pa = MM[ma][:, :].rearrange("p (s c) -> p s c", s=2)[:, :, 0:w]
                    pb = MM[mb][:, :].rearrange("p (s c) -> p s c", s=2)[:, :, 0:w]
                    S.op("dve", lambda: nc.vector.tensor_tensor(out=v3(tmp), in0=v3(zt), in1=pa, op=ALU.mult),
                         reads=x32k(n, c0, ns) + [f"mm:{ma}:0", f"mm:{ma}:1"], writes=tk)
                    S.op("dve", lambda: nc.vector.tensor_tensor(out=v3(tmp), in0=v3(tmp), in1=pb, op=ALU.add),
                         reads=tk + [f"mm:{mb}:0", f"mm:{mb}:1"], writes=tk)
                    pieces_ = [(tmp, zt, tk, c0, ns)]
                else:
                    pieces_ = []
                    for si, sg in enumerate(segs):
                        tmp = AR[:, tc + sg.c0:tc + sg.c0 + sg.n]
                        zt = x32T[:, n, sg.c0:sg.c0 + sg.n]
                        tk = ark(tc + sg.c0, sg.n)
                        S.op("dve", lambda: nc.vector.tensor_tensor(out=tmp, in0=zt, in1=mmv(ma, si, sg), op=ALU.mult),
                             reads=x32k(n, sg.c0, sg.n) + [f"mm:{ma}:{si}"], writes=tk)
                        S.op("dve", lambda: nc.vector.tensor_tensor(out=tmp, in0=tmp, in1=mmv(mb, si, sg), op=ALU.add),
                             reads=tk + [f"mm:{mb}:{si}"], writes=tk)
                        pieces_.append((tmp, zt, tk, sg.c0, sg.n))
                for (tmp, zt, tk, c0, ns) in pieces_:
                    if write_xT:
                        S.op("act", lambda: nc.scalar.activation(out=xT[:, n, c0:c0 + ns], in_=tmp,
                                                                 func=AF.Identity, scale=gsc, bias=bsc),
                             reads=tk + ["gcol", "bcol"], writes=xTk(n, c0, ns))
                    S.op("pool", lambda: nc.gpsimd.tensor_scalar(out=zt, in0=tmp, scalar1=gsc, scalar2=bsc,
                                                                 op0=ALU.mult, op1=ALU.add),
                         reads=tk + ["gcol", "bcol"], writes=x32k(n, c0, ns))

        def resid_evac(m, n, segs, first=True):
            for si, sg in enumerate(segs):
                zt = x32T[:, n, sg.c0:sg.c0 + sg.n]
                if first:
                    S.op("dve", lambda: nc.vector.scalar_tensor_tensor(out=zt, in0=zt, scalar=ALPHA, in1=mmv(m, si, sg),
                                                                       op0=ALU.mult, op1=ALU.add),
                         reads=x32k(n, sg.c0, sg.n) + [f"mm:{m}:{si}"], writes=x32k(n, sg.c0, sg.n))
                else:
                    S.op("dve", lambda: nc.vector.tensor_tensor(out=zt, in0=zt, in1=mmv(m, si, sg), op=ALU.add),
                         reads=x32k(n, sg.c0, sg.n) + [f"mm:{m}:{si}"], writes=x32k(n, sg.c0, sg.n))

        def ffn(layer, segs, tiles):
            gu, dn = f"w_gu{layer}", f"w_dn{layer}"
            sgc = [2304, 2304 + 576]
            for hf in range(2):
                for jj in range(FH):
                    j = hf * FH + jj
                    mg = proj(gu, j * 128, 0, KC, x_rhs, x_keys, segs)
                    mu = proj(gu, DFF + j * 128, 0, KC, x_rhs, x_keys, segs)
                    sc = sgc[jj % 2]
                    for si, sg in enumerate(segs):
                        sgt = AR[:, sc + sg.c0:sc + sg.c0 + sg.n]
                        S.op("act", lambda: nc.scalar.activation(out=sgt, in_=mmv(mg, si, sg), func=AF.Silu),
                             reads=[f"mm:{mg}:{si}"], writes=ark(sc + sg.c0, sg.n))
                        S.op("dve", lambda: nc.vector.tensor_tensor(out=R[:, jj, sg.c0:sg.c0 + sg.n], in0=mmv(mu, si, sg),
                                                                    in1=sgt, op=ALU.mult),
                             reads=[f"mm:{mu}:{si}"] + ark(sc + sg.c0, sg.n), writes=rk(jj, sg))
                proj_resid_ln(dn, r_rhs, r_keys, segs, tiles, k0=hf * FH, nk=FH, first=(hf == 0), stats=(hf == 1))

        def mixer(g, segs, chsegs, stream):
            cC, cU, cV = 0, 640, 1280

            def uoff(pc0, kind):
                if g == 0:
                    return pc0
                return 2 + pc0 if kind == "p" else 516 + (pc0 - 512)

            def voff(pc0, kind):
                if g == 0:
                    return pc0 - 2
                return pc0 if kind == "p" else 514 + (pc0 - 512)

            for j in range(KC):
                mc = proj("w_in", D + j * 128, 0, KC, x_rhs, x_keys, chsegs)
                mh = proj("w_in", 2 * D + j * 128, 0, KC, x_rhs, x_keys, chsegs)
                mb_ = proj("w_in", j * 128, 0, KC, x_rhs, x_keys, segs)
                if g > 0:
                    S.op("act", lambda: nc.scalar.copy(AR[:, cU:cU + 2], carry[:, :, j]),
                         reads=["carry"], writes=ark(cU, 2))
                    S.op("act", lambda: nc.scalar.copy(AR[:, cU + 514:cU + 516],
                                                       shT[:, stream, :].rearrange("p (r k) -> p r k", r=2)[:, :, j]),
                         reads=[f"shT:{stream}"], writes=ark(cU + 514, 2))
                for si, sg in enumerate(chsegs):
                    for (pc0, pn, kind) in pieces(sg, g):
                        uo = uoff(pc0, kind)
                        S.op("act", lambda: nc.scalar.copy(AR[:, cC + uo:cC + uo + pn], mmp(mc, si, sg, pc0, pn)),
                             reads=[f"mm:{mc}:{si}"], writes=ark(cC + uo, pn))
                        S.op("dve", lambda: nc.vector.tensor_tensor(out=AR[:, cU + uo:cU + uo + pn], in0=mmp(mh, si, sg, pc0, pn),
                                                                    in1=AR[:, cC + uo:cC + uo + pn], op=ALU.mult),
                             reads=[f"mm:{mh}:{si}"] + ark(cC + uo, pn), writes=ark(cU + uo, pn))
                L = 512 if g == 0 else 578
                S.op("act", lambda: nc.scalar.activation(out=AR[:, cV:cV + L], in_=AR[:, cU:cU + L], func=AF.Identity,
                                                         scale=cw[:, j:j + 1]),
                     reads=ark(cU, L + 2) + ["cw"], writes=ark(cV, L))
                for tap in (1, 2):
                    S.op("dve", lambda: nc.vector.scalar_tensor_tensor(
                        out=AR[:, cV:cV + L], in0=AR[:, cU + tap:cU + tap + L], scalar=cw[:, tap * 16 + j:tap * 16 + j + 1],
                        in1=AR[:, cV:cV + L], op0=ALU.mult, op1=ALU.add),
                        reads=ark(cU, L + 2) + ark(cV, L) + ["cw"], writes=ark(cV, L))
                for si, sg in enumerate(segs):
                    for (pc0, pn, kind) in pieces(sg, g):
                        vo = voff(pc0, kind)
                        S.op("dve", lambda: nc.vector.tensor_tensor(out=R[:, j, pc0:pc0 + pn], in0=mmp(mb_, si, sg, pc0, pn),
                                                                    in1=AR[:, cV + vo:cV + vo + pn], op=ALU.mult),
                             reads=[f"mm:{mb_}:{si}"] + ark(cV + vo, pn), writes=[f"R:{j}:{t}" for t in tl(pc0, pn)])
                S.op("act", lambda: nc.scalar.copy(carry[:, :, j], AR[:, cU + 512:cU + 514]),
                     reads=ark(cU + 512, 2), writes=["carry"])
                if g > 0:
                    S.op("act", lambda: nc.scalar.copy(sstate[:, :, j], AR[:, cU + 578:cU + 580]),
                         reads=ark(cU + 578, 2), writes=["sstate"])

        def store_state(src, key, dst):
            t = next_tp()
            S.op("pe", lambda: nc.tensor.transpose(TP[t][0:32, 0:128], src[:].rearrange("p r k -> p (r k)"), ident[:, :]),
                 reads=[key, "ident"], writes=[f"tp:{t}"])
            sg = next_stg()
            S.op("dve", lambda: nc.vector.tensor_copy(STG[sg][0:32, 0:128], TP[t][0:32, 0:128]),
                 reads=[f"tp:{t}"], writes=[f"stg:{sg}"])
            S.dma("sp", f"g{sg}", dst, STG[sg][0:32, 0:128], reads=[f"stg:{sg}"], writes=[f"out:{key}"])

        def kv_proj(g, segs, own, stream):
            cFb = [3456, 4032]
            out_p = (g == 2)
            out_s = (g > 0)
            toff = 2 if g == 0 else 0

            def rows_out(which, h, kind, cF):
                if kind == "p":
                    dstp = k_p if which == "k" else v_p
                    t = next_tp()
                    for i in range(4):
                        S.op("pe", lambda: nc.tensor.transpose(TP[t][:, i * 128:(i + 1) * 128],
                                                               AR[:, cF + i * 128:cF + (i + 1) * 128], ident[:, :]),
                             reads=ark(cF + i * 128, 128) + ["ident"], writes=[f"tp:{t}"], signal=(i == 3))
                    s2 = next_stg()
                    e = ev_eng()
                    S.op(e, lambda: copy_on(e, STG[s2][:, :], TP[t][:, :]), reads=[f"tp:{t}"], writes=[f"stg:{s2}"])
                    S.dma("sp", f"g{s2}", dstp[:].rearrange("(t p) n -> p t n", p=128)[:, :, h * 128:(h + 1) * 128],
                          STG[s2][:, :].rearrange("p (t d) -> p t d", t=4), reads=[f"stg:{s2}"], writes=[f"out:{which}p"])
                else:
                    dsts = k_s if which == "k" else v_s
                    t = next_tp()
                    S.op("pe", lambda: nc.tensor.transpose(TP[t][0:64, 0:128], AR[:, cF + 512:cF + 576], ident[:, :]),
                         reads=ark(cF + 512, 64) + ["ident"], writes=[f"tp:{t}"])
                    s2 = next_stg()
                    e = ev_eng()
                    S.op(e, lambda: copy_on(e, STG[s2][0:64, 0:128], TP[t][0:64, 0:128]),
                         reads=[f"tp:{t}"], writes=[f"stg:{s2}"])
                    S.dma("sp", f"g{s2}", dsts[stream * 64:(stream + 1) * 64, h * 128:(h + 1) * 128], STG[s2][0:64, 0:128],
                          reads=[f"stg:{s2}"], writes=[f"out:{which}s"])

            def k_a(h):
                cF = cFb[h % 2]
                m = proj("w_kv", h * 128, 0, KC, x_rhs, x_keys, segs)
                for si, sg in enumerate(segs):
                    for (pc0, pn, kind) in pieces(sg, g):
                        tk0 = pc0 - toff
                        if kind == "p":
                            dst, key = KT[:, h, own * 512 + tk0:own * 512 + tk0 + pn], f"KT:{own}:{h}"
                        else:
                            dst, key = KTs[:, h, tk0 - 512:tk0 - 512 + pn], f"KTs:{h}"
                        S.op("act", lambda: nc.scalar.copy(dst, mmp(m, si, sg, pc0, pn)), reads=[f"mm:{m}:{si}"], writes=[key])
                        if (kind == "p" and out_p) or (kind == "s" and out_s):
                            S.op("dve", lambda: nc.vector.tensor_copy(AR[:, cF + tk0:cF + tk0 + pn], mmp(m, si, sg, pc0, pn)),
                                 reads=[f"mm:{m}:{si}"], writes=ark(cF + tk0, pn))

            def k_b(h):
                cF = cFb[h % 2]
                if out_p:
                    rows_out("k", h, "p", cF)
                if out_s:
                    rows_out("k", h, "s", cF)

            def v_a(h):
                cF = cFb[h % 2]
                m = proj("w_kv", D + h * 128, 0, KC, x_rhs, x_keys, segs)
                for si, sg in enumerate(segs):
                    for (pc0, pn, kind) in pieces(sg, g):
                        tk0 = pc0 - toff
                        S.op("dve", lambda: nc.vector.tensor_copy(AR[:, cF + tk0:cF + tk0 + pn], mmp(m, si, sg, pc0, pn)),
                             reads=[f"mm:{m}:{si}"], writes=ark(cF + tk0, pn))

            def v_b(h):
                cF = cFb[h % 2]
                t = next_tp()
                for i in range(4):
                    S.op("pe", lambda: nc.tensor.transpose(TP[t][:, i * 128:(i + 1) * 128],
                                                           AR[:, cF + i * 128:cF + (i + 1) * 128], ident[:, :]),
                         reads=ark(cF + i * 128, 128) + ["ident"], writes=[f"tp:{t}"], signal=(i == 3))
                vdst = V[:, own * 4:own * 4 + 4, h * 128:(h + 1) * 128]
                tpv = TP[t][:, :].rearrange("p (t d) -> p t d", t=4)
                vkeys = [f"V:{own * 4 + i}:{h}" for i in range(4)]
                if g == 0:
                    S.op("act", lambda: nc.scalar.activation(out=vdst, in_=tpv, func=AF.Identity, scale=hvt[:]),
                         reads=[f"tp:{t}", "hvt"], writes=vkeys)
                else:
                    S.op("act", lambda: nc.scalar.copy(vdst, tpv), reads=[f"tp:{t}"], writes=vkeys)
                if out_p:
                    s2 = next_stg()
                    S.op("dve", lambda: nc.vector.tensor_copy(STG[s2][:, :], TP[t][:, :]),
                         reads=[f"tp:{t}"], writes=[f"stg:{s2}"])
                    S.dma("sp", f"g{s2}", v_p[:].rearrange("(t p) n -> p t n", p=128)[:, :, h * 128:(h + 1) * 128],
                          STG[s2][:, :].rearrange("p (t d) -> p t d", t=4), reads=[f"stg:{s2}"], writes=["out:vp"])
                if g > 0:
                    t = next_tp()
                    S.op("pe", lambda: nc.tensor.transpose(TP[t][0:64, 0:128], AR[:, cF + 512:cF + 576], ident[:, :]),
                         reads=ark(cF + 512, 64) + ["ident"], writes=[f"tp:{t}"])
                    S.op("act", lambda: nc.scalar.copy(Vs[0:64, h * 128:(h + 1) * 128], TP[t][0:64, 0:128]),
                         reads=[f"tp:{t}"], writes=[f"Vs:{h}"])
                    s2 = next_stg()
                    S.op("dve", lambda: nc.vector.tensor_copy(STG[s2][0:64, 0:128], TP[t][0:64, 0:128]),
                         reads=[f"tp:{t}"], writes=[f"stg:{s2}"])
                    S.dma("sp", f"g{s2}", v_s[stream * 64:(stream + 1) * 64, h * 128:(h + 1) * 128], STG[s2][0:64, 0:128],
                          reads=[f"stg:{s2}"], writes=["out:vs"])

            for fa, fb in ((k_a, k_b), (v_a, v_b)):
                fa(0)
                for h in range(NH):
                    if h + 1 < NH:
                        fa(h + 1)
                    fb(h)

        def attention(g, own, stream):
            prev = 1 - own
            cE = [0, 640]
            cH = [1280, 1920]
            cKS = [2560, 3072]
            cVS = [3584, 4096]
            cRI = [4608, 4608]

            def prologue(h):
                b = h % 2
                S.dma("sp", f"hk{b}", AR[:, cH[b]:cH[b] + 640], bass.AP(ebias, h * 768, [[1, 128], [1, 640]]),
                      reads=["ebias"], writes=ark(cH[b], 640))
                S.op("pool", lambda: nc.gpsimd.memset(AR[0:64, cH[b]:cH[b] + 64], 0.0), writes=ark(cH[b], 64))
                S.op("pool", lambda: nc.gpsimd.memset(AR[64:128, cH[b] + 576:cH[b] + 640], 0.0), writes=ark(cH[b] + 576, 64))
                S.dma("sp", f"ks{b}", AR[:, cKS[b]:cKS[b] + 512].rearrange("p (t d) -> p t d", t=4),
                      ck[stream].rearrange("(t p) n -> p t n", p=128)[:, :, h * 128:(h + 1) * 128],
                      writes=ark(cKS[b], 512))
                S.dma("sp", f"vs{b}", AR[:, cVS[b]:cVS[b] + 512].rearrange("p (t d) -> p t d", t=4),
                      cv[stream].rearrange("(t p) n -> p t n", p=128)[:, :, h * 128:(h + 1) * 128],
                      writes=ark(cVS[b], 512))
                t = next_tp()
                for i in range(4):
                    S.op("pe", lambda: nc.tensor.transpose(TP[t][:, i * 128:(i + 1) * 128],
                                                           AR[:, cKS[b] + i * 128:cKS[b] + (i + 1) * 128], ident[:, :]),
                         reads=ark(cKS[b] + i * 128, 128) + ["ident"], writes=[f"tp:{t}"], signal=(i == 3))
                S.op("act", lambda: nc.scalar.copy(KTc[b][:, :], TP[t][:, :]), reads=[f"tp:{t}"], writes=[f"KTc:{b}"])
                S.op("pool", lambda: nc.gpsimd.tensor_copy(Vc[b][:, :, :], AR[:, cVS[b]:cVS[b] + 512].rearrange("p (t d) -> p t d", t=4)),
                     reads=ark(cVS[b], 512), writes=[f"Vc:{b}"])

            def stage_a(w):
                gi, h, kind, t_ = w
                b, eb = h % 2, gi % 2
                m = next_mm()
                qk = f"R:{h}:{t_}"
                if kind == "p":
                    q_ap = R[:, h, t_ * 128:(t_ + 1) * 128]
                    for mm_ in range(5):
                        c = t_ + mm_
                        half, cc = (prev, c) if c < 4 else (own, c - 4)
                        S.op("pe", lambda: nc.tensor.matmul(MM[m][:, mm_ * 128:(mm_ + 1) * 128],
                                                            KT[:, h, half * 512 + cc * 128: half * 512 + (cc + 1) * 128],
                                                            q_ap, start=True, stop=True),
                             reads=[f"KT:{half}:{h}", qk], writes=[f"mm:{m}:{mm_ // 4}"], signal=(mm_ == 4))
                    S.op("act", lambda: nc.scalar.activation(out=AR[:, cE[eb]:cE[eb] + 640], in_=MM[m][:, 0:640], func=AF.Exp),
                         reads=[f"mm:{m}:0", f"mm:{m}:1"], writes=ark(cE[eb], 640))
                    hrev = bass.AP(AR, cH[b] + 127, [[ARC, 128], [128, 5], [-1, 128]])
                    S.op("dve", lambda: nc.vector.tensor_tensor(
                        out=PT[eb][:, :].rearrange("p (m q) -> p m q", m=5),
                        in0=AR[:, cE[eb]:cE[eb] + 640].rearrange("p (m q) -> p m q", m=5), in1=hrev, op=ALU.mult),
                        reads=ark(cE[eb], 640) + ark(cH[b], 640), writes=[f"PT:{eb}"])
                else:
                    q_ap = R[:, h, 512:576]
                    for mm_ in range(4):
                        S.op("pe", lambda: nc.tensor.matmul(MM[m][:, mm_ * 64:(mm_ + 1) * 64],
                                                            KTc[b][:, mm_ * 128:(mm_ + 1) * 128], q_ap, start=True, stop=True),
                             reads=[f"KTc:{b}", qk], writes=[f"mm:{m}:0"], signal=False)
                    S.op("pe", lambda: nc.tensor.matmul(MM[m][0:64, 256:320], KTs[:, h, :], q_ap, start=True, stop=True),
                         reads=[f"KTs:{h}", qk], writes=[f"mm:{m}:0"])
                    S.op("act", lambda: nc.scalar.activation(out=AR[:, cE[eb]:cE[eb] + 256], in_=MM[m][:, 0:256], func=AF.Exp),
                         reads=[f"mm:{m}:0"], writes=ark(cE[eb], 256))
                    S.op("act", lambda: nc.scalar.activation(out=AR[0:64, cE[eb] + 256:cE[eb] + 320], in_=MM[m][0:64, 256:320], func=AF.Exp),
                         reads=[f"mm:{m}:0"], writes=ark(cE[eb] + 256, 64))
                    hrev = bass.AP(AR, cH[b] + 127, [[ARC, 128], [128, 4], [-1, 64]])
                    S.op("dve", lambda: nc.vector.tensor_tensor(
                        out=PT[eb][:, 0:256].rearrange("p (m q) -> p m q", m=4),
                        in0=AR[:, cE[eb]:cE[eb] + 256].rearrange("p (m q) -> p m q", m=4), in1=hrev, op=ALU.mult),
                        reads=ark(cE[eb], 256) + ark(cH[b], 640), writes=[f"PT:{eb}"])
                    hrev4 = bass.AP(AR, cH[b] + 512 + 127, [[ARC, 64], [-1, 64]])
                    S.op("dve", lambda: nc.vector.tensor_tensor(
                        out=PT[eb][0:64, 256:320], in0=AR[0:64, cE[eb] + 256:cE[eb] + 320], in1=hrev4, op=ALU.mult),
                        reads=ark(cE[eb] + 256, 64) + ark(cH[b], 640), writes=[f"PT:{eb}"])

            def stage_b(w):
                gi, h, kind, t_ = w
                b, eb = h % 2, gi % 2
                qk = f"R:{h}:{t_}"
                NQ = 128 if kind == "p" else 64
                q_ap = R[:, h, t_ * 128:(t_ + 1) * 128] if kind == "p" else R[:, h, 512:576]
                mo = next_mm()
                nchunk = 5
                for pass_ in range(2):
                    for mm_ in range(nchunk):
                        if kind == "p":
                            c = t_ + mm_
                            half, cc = (prev, c) if c < 4 else (own, c - 4)
                            halo = (g == 1 and c < 4)
                            if pass_ == 0:
                                lhs = V[:, half * 4 + cc, h * 128:(h + 1) * 128]
                                lk = [f"V:{half * 4 + cc}:{h}"]
                            else:
                                lhs = ones_hv[:, :] if halo else ones_b[:, :]
                                lk = ["ones_hv" if halo else "ones_b"]
                            rhs = PT[eb][:, mm_ * 128:(mm_ + 1) * 128]
                        else:
                            if mm_ < 4:
                                lhs = Vc[b][:, mm_, :] if pass_ == 0 else ones_b[:, :]
                                lk = [f"Vc:{b}"] if pass_ == 0 else ["ones_b"]
                                rhs = PT[eb][:, mm_ * 64:(mm_ + 1) * 64]
                            else:
                                lhs = Vs[0:64, h * 128:(h + 1) * 128] if pass_ == 0 else ones_b[0:64, :]
                                lk = [f"Vs:{h}"] if pass_ == 0 else ["ones_b"]
                                rhs = PT[eb][0:64, 256:320]
                        S.op("pe", lambda: nc.tensor.matmul(MM[mo][:, pass_ * 128:pass_ * 128 + NQ], lhs, rhs,
                                                            start=(mm_ == 0), stop=(mm_ == nchunk - 1)),
                             reads=lk + [f"PT:{eb}"], writes=[f"mm:{mo}:0"], signal=(pass_ == 1 and mm_ == nchunk - 1))
                ri = AR[:, cRI[eb]:cRI[eb] + NQ]
                S.op("dve", lambda: nc.vector.reciprocal(ri, MM[mo][:, 128:128 + NQ]),
                     reads=[f"mm:{mo}:0"], writes=ark(cRI[eb], NQ))
                S.op("dve", lambda: nc.vector.tensor_tensor(out=q_ap, in0=MM[mo][:, 0:NQ], in1=ri, op=ALU.mult),
                     reads=[f"mm:{mo}:0"] + ark(cRI[eb], NQ), writes=[qk])

            work = []
            for h in range(NH):
                for kind, t_ in [("p", 0), ("p", 1), ("p", 2), ("p", 3), ("s", 4)]:
                    work.append((len(work), h, kind, t_))
            prologue(0)
            stage_a(work[0])
            for i, w in enumerate(work):
                if i + 1 < len(work):
                    nx = work[i + 1]
                    if nx[1] != w[1]:
                        prologue(nx[1])
                    stage_a(nx)
                stage_b(w)

        def _emit_groups():
            for g in range(3):
                if g == 0:
                    segs = [Seg(2, 512)]
                    chsegs = [Seg(0, 257), Seg(257, 257)]
                    tiles = [(2 + 128 * i, 128) for i in range(4)]
                    own, stream = 0, 0
                    load_x(xp, 0, 2, 0)
                    for i in range(4):
                        load_x(xp, 2 + 128 * i, 128, 2 + 128 * i)
                else:
                    segs = [Seg(0, 288), Seg(288, 288)]
                    chsegs = segs
                    tiles = [(128 * i, 128) for i in range(4)] + [(512, 64)]
                    own, stream = (1, 0) if g == 1 else (0, 1)
                    for i in range(4):
                        load_x(xp, 514 + 512 * (g - 1) + 128 * i, 128, 128 * i)
                    load_x(xs, 64 * stream, 64, 512)

                _chk(f"g{g}_load")
                mixer(g, segs, chsegs, stream)
                _chk(f"g{g}_mixer")
                if g > 0:
                    store_state(sstate, "sstate", conv_s[stream])
                if g == 2:
                    store_state(carry, "carry", conv_p[:])
                proj_resid_ln("w_out", r_rhs, r_keys, segs, tiles)
                _chk(f"g{g}_wout")
                layer_norm(0, segs, tiles, True)
                _chk(f"g{g}_ln0")
                ffn(0, segs, tiles)
                _chk(f"g{g}_ffn0")
                layer_norm(1, segs, tiles, True)
                _chk(f"g{g}_ln1")
                kv_proj(g, segs, own, stream)
                _chk(f"g{g}_kv")
                if g == 0:
                    continue
                for h in range(NH):
                    m = proj("w_q", h * 128, 0, KC, x_rhs, x_keys, segs)
                    for si, sg in enumerate(segs):
                        S.op("act", lambda: nc.scalar.activation(out=R[:, h, sg.c0:sg.c0 + sg.n], in_=mmv(m, si, sg),
                                                                 func=AF.Identity, scale=QSCALE),
                             reads=[f"mm:{m}:{si}"], writes=rk(h, sg))
                _chk(f"g{g}_q")
                attention(g, own, stream)
                _chk(f"g{g}_attn")
                proj_resid_ln("w_o", r_rhs, r_keys, segs, tiles)
                layer_norm(2, segs, tiles, True)
                ffn(1, segs, tiles)
                layer_norm(3, segs, tiles, False)
                store_tiles([(y_p, 512 * (g - 1) + 128 * i, 128, 128 * i) for i in range(4)] + [(y_s, 64 * stream, 64, 512)])


        try:
            _chk("init")
            _emit_groups()
        except _Stop:
            pass
        for i_ in range(NSTG):
            S.eng["sp"].wait_ge(S.sem[f"g{i_}"], 16 * S.n[f"g{i_}"])
        build_program.stats = (S.n_ins, S.n_wait, dict(S.n))
    return nc


def _layout_inputs(x_prompt, x_sample, state_conv, cache_k, cache_v, w_in_a, conv_w, w_out_a,
                   w_kv, w_q, w_o, rel_bias, ln_g, ln_b, w_gate_up, w_down):
    f = lambda a: np.ascontiguousarray(np.asarray(a, dtype=np.float32))
    xp_full = f(x_prompt)[0]
    xs_full = f(x_sample).reshape(16 * 64, D)
    sc = f(state_conv)[0]
    ckf = f(cache_k).reshape(16, 512, D)
    cvf = f(cache_v).reshape(16, 512, D)
    shared = {
        "w_in": f(w_in_a)[0], "conv_w": f(conv_w)[0].reshape(48, 128), "w_out": f(w_out_a)[0],
        "w_kv": f(w_kv), "w_q": f(w_q)[0], "w_o": f(w_o)[0], "rel_bias": f(rel_bias)[0],
        "ln_g": f(ln_g).reshape(64, 128), "ln_b": f(ln_b).reshape(64, 128),
        "w_gu0": f(w_gate_up)[0], "w_gu1": f(w_gate_up)[1], "w_dn0": f(w_down)[0], "w_dn1": f(w_down)[1],
    }
    padded = np.concatenate([np.zeros((514, D), np.float32), xp_full], axis=0)
    in_maps = []
    for i in range(NCORES):
        m = dict(shared)
        m["xp"] = np.ascontiguousarray(padded[1024 * i: 1024 * i + 1538])
        m["xs"] = np.ascontiguousarray(xs_full[128 * i: 128 * i + 128])
        m["sconv"] = np.ascontiguousarray(sc[2 * i: 2 * i + 2].reshape(2, 32, 128))
        m["ck"] = np.ascontiguousarray(ckf[2 * i: 2 * i + 2])
        m["cv"] = np.ascontiguousarray(cvf[2 * i: 2 * i + 2])
        m["hv"] = np.full((128, 1), 0.0 if i == 0 else 1.0, np.float32)
        in_maps.append(m)
    return in_maps


def kernel(**inputs):
    in_maps = _layout_inputs(**inputs)
    nc = build_program()
    res = run_bass_kernel_spmd(nc, in_maps, core_ids=list(range(NCORES)))
    r = res.results
    y_prompt = np.concatenate([r[i]["y_p"] for i in range(NCORES)], axis=0).reshape(1, 8192, D)
    y_sample = np.concatenate([r[i]["y_s"] for i in range(NCORES)], axis=0).reshape(16, 64, D)
    conv_prompt = r[NCORES - 1]["conv_p"].reshape(1, 1, 2, D)
    conv_sample = np.concatenate([r[i]["conv_s"].reshape(2, 2, D) for i in range(NCORES)], axis=0).reshape(1, 16, 2, D)
    k_prompt = r[NCORES - 1]["k_p"].reshape(1, 512, NH, 128)
    v_prompt = r[NCORES - 1]["v_p"].reshape(1, 512, NH, 128)
    k_sample = np.concatenate([r[i]["k_s"] for i in range(NCORES)], axis=0).reshape(16, 64, NH, 128)
    v_sample = np.concatenate([r[i]["v_s"] for i in range(NCORES)], axis=0).reshape(16, 64, NH, 128)
    outs = (y_prompt, y_sample, conv_prompt, conv_sample, k_prompt, v_prompt, k_sample, v_sample)
    return tuple(np.ascontiguousarray(o, dtype=np.float32) for o in outs)
```

```python
import contextlib
import numpy as np
import concourse.bass as bass
import concourse.mybir as mybir
from concourse.bass_utils import run_bass_kernel_spmd

F32 = mybir.dt.float32
BF16 = mybir.dt.bfloat16
AF = mybir.ActivationFunctionType
ALU = mybir.AluOpType

D = 2048
KC = 16
DFF = 5632
FC = 44
FH = 22
NH = 16
ALPHA = 4.0 ** 0.25
EPS = 1e-5
QSCALE = 128.0 ** -0.5
NCORES = 8
GW = 576
NSLOT = 4
ARC = 4736
SMALLW = False
STOP = None


class _Stop(Exception):
    pass


def _chk(tag):
    if STOP == tag:
        raise _Stop()


class Sched:
    def __init__(self, nc, sems, dma_channels):
        self.nc = nc
        self.eng = {"pe": nc.tensor, "dve": nc.vector, "act": nc.scalar,
                    "pool": nc.gpsimd, "sp": nc.sync}
        self.sem = dict(sems)
        self.n = {k: 0 for k in self.sem}
        self.sigs = {k: [] for k in self.sem}
        self.waited = {}
        self.last_w = {}
        self.readers = {}
        self.dma_channels = set(dma_channels)
        self.n_wait = 0
        self.n_ins = 0

    def _sem_value(self, e, idx):
        if e in self.dma_channels:
            return 16 * (idx + 1)
        s = self.sigs[e]
        lo, hi = 0, len(s)
        while lo < hi:
            mid = (lo + hi) // 2
            if s[mid] >= idx:
                hi = mid
            else:
                lo = mid + 1
        assert lo < len(s), f"no signalled instr on {e} at/after {idx}"
        return lo + 1

    def _deps(self, reads, writes):
        deps = []
        lw = self.last_w
        for k in reads:
            w = lw.get(k)
            if w is not None:
                deps.append(w)
        for k in writes:
            w = lw.get(k)
            if w is not None:
                deps.append(w)
            r = self.readers.get(k)
            if r:
                deps.extend(r.items())
        return deps

    def _emit_waits(self, on, deps, skip_same=False):
        best = {}
        for e, i in deps:
            if skip_same and e == on:
                continue
            if e not in best or i > best[e]:
                best[e] = i
        for e, i in best.items():
            v = self._sem_value(e, i)
            if self.waited.get((on, e), 0) >= v:
                continue
            self.eng[on].wait_ge(self.sem[e], v)
            self.waited[(on, e)] = v
            self.n_wait += 1

    def _record(self, who, idx, reads, writes):
        for k in reads:
            self.readers.setdefault(k, {})[who] = idx
        for k in writes:
            self.last_w[k] = (who, idx)
            self.readers[k] = {}

    def op(self, on, ins_fn, reads=(), writes=(), signal=True):
        ex = [k for k in reads if k[:3] in ("mm:", "tp:")]
        if ex and on != "pe":
            writes = list(writes) + ex
        self._emit_waits(on, self._deps(reads, writes), skip_same=(on == "pe"))
        ins = ins_fn()
        idx = self.n[on]
        self.n[on] += 1
        self.n_ins += 1
        if signal:
            ins.then_inc(self.sem[on], 1)
            self.sigs[on].append(idx)
        self._record(on, idx, reads, writes)
        return ins

    def dma(self, queue, chan, out, in_, reads=(), writes=()):
        self._emit_waits(queue, self._deps(reads, writes))
        ins = self.eng[queue].dma_start(out=out, in_=in_)
        ins.then_inc(self.sem[chan], 16)
        idx = self.n[chan]
        self.n[chan] += 1
        self.n_ins += 1
        self._record(chan, idx, reads, writes)
        return ins

    def wait_all(self, on, keys):
        self._emit_waits(on, [self.last_w[k] for k in keys if k in self.last_w])


def build_program():
    nc = bass.Bass("TRN2", target_bir_lowering=False)

    def din(name, shape):
        return nc.dram_tensor(name, list(shape), F32, kind="ExternalInput")

    def dout(name, shape):
        return nc.dram_tensor(name, list(shape), F32, kind="ExternalOutput")

    xp = din("xp", [1538, D])
    xs = din("xs", [128, D])
    sconv = din("sconv", [2, 32, 128])
    ck = din("ck", [2, 512, D])
    cv = din("cv", [2, 512, D])
    hv = din("hv", [128, 1])
    if SMALLW:
        _din = din
        din = lambda name, shape: _din(name, [128, 128] if name.startswith("w_") else shape)
    w_in = din("w_in", [D, 3 * D])
    conv_w = din("conv_w", [48, 128])
    w_out = din("w_out", [D, D])
    w_kv = din("w_kv", [D, 2 * D])
    w_q = din("w_q", [D, D])
    w_o = din("w_o", [D, D])
    rel_bias = din("rel_bias", [NH, 513])
    ln_g = din("ln_g", [64, 128])
    ln_b = din("ln_b", [64, 128])
    w_gu = [din("w_gu0", [D, 2 * DFF]), din("w_gu1", [D, 2 * DFF])]
    w_dn = [din("w_dn0", [DFF, D]), din("w_dn1", [DFF, D])]

    y_p = dout("y_p", [1024, D])
    y_s = dout("y_s", [128, D])
    conv_p = dout("conv_p", [32, 128])
    conv_s = dout("conv_s", [2, 32, 128])
    k_p = dout("k_p", [512, D])
    v_p = dout("v_p", [512, D])
    k_s = dout("k_s", [128, D])
    v_s = dout("v_s", [128, D])
    ebias = nc.dram_tensor("ebias", [NH, 768], F32)

    dma_chans = ([f"w{i}" for i in range(NSLOT)] + [f"g{i}" for i in range(4)]
                 + ["hk0", "hk1", "ks0", "ks1", "vs0", "vs1", "hv", "rb", "eb"])
    sem_names = ["pe", "dve", "act", "pool", "sp"] + dma_chans
    with contextlib.ExitStack() as st:
        sems = {n: st.enter_context(nc.semaphore(n)) for n in sem_names}
        S = Sched(nc, sems, dma_chans)

        def sb(name, shape, dt):
            return st.enter_context(nc.sbuf_tensor(name, list(shape), dt))

        def ps(name, shape, dt=F32):
            return st.enter_context(nc.psum_tensor(name, list(shape), dt))

        x32T = sb("x32T", [128, KC, GW], F32)
        xT = sb("xT", [128, KC, GW], BF16)
        KT = sb("KT", [128, NH, 1024], BF16)
        KTs = sb("KTs", [128, NH, 64], BF16)
        V = sb("V", [128, 8, D], BF16)
        Vs = sb("Vs", [128, D], BF16)
        R = sb("R", [128, FH, GW], BF16)
        WR = [sb(f"WR{i}", [128, FH, 128], BF16) for i in range(NSLOT)]
        STG = [sb(f"STG{i}", [128, 512], F32) for i in range(4)]
        AR = sb("AR", [128, ARC], F32)
        PT = [sb(f"PT{i}", [128, 640], BF16) for i in range(2)]
        KTc = [sb(f"KTc{i}", [128, 512], BF16) for i in range(2)]
        Vc = [sb(f"Vc{i}", [128, 4, 128], BF16) for i in range(2)]
        ident = sb("ident", [128, 128], F32)
        ones_f = sb("ones_f", [128, 128], F32)
        ones_b = sb("ones_b", [128, 128], BF16)
        ones_hv = sb("ones_hv", [128, 128], BF16)
        sel = sb("sel", [2, 2, 128], F32)
        stats = sb("stats", [128, 4, 6], F32)
        mv = sb("mv", [128, 5, 2], F32)
        rs2 = sb("rs2", [128, 5, 2], F32)
        std = sb("std", [128, 5], F32)
        gcol = sb("gcol", [128, 64], F32)
        bcol = sb("bcol", [128, 64], F32)
        cw = sb("cw", [128, 48], F32)
        carry = sb("carry", [128, 2, KC], F32)
        shT = sb("shT", [128, 2, 32], F32)
        sstate = sb("sstate", [128, 2, KC], F32)
        hvt = sb("hvt", [128, 1], F32)

        MM = [ps(f"MM{i}", [128, 1024]) for i in range(3)]
        TP = [ps(f"TP{i}", [128, 512]) for i in range(2)]

        def ar(c0, n):
            return AR[:, c0:c0 + n]

        def ark(c0, n):
            return [f"ar:{p}" for p in range(c0 // 64, (c0 + n - 1) // 64 + 1)]

        cnt = {"mm": 0, "tp": 0, "stg": 0, "w": 0, "ev": 0}

        def next_mm():
            s = cnt["mm"] % 3
            cnt["mm"] += 1
            return s

        def next_tp():
            s = cnt["tp"] % 2
            cnt["tp"] += 1
            return s

        def next_stg():
            s = cnt["stg"] % 4
            cnt["stg"] += 1
            return s

        def ev_eng():
            cnt["ev"] += 1
            return "act" if cnt["ev"] % 2 else "dve"

        def copy_on(eng, out, in_):
            if eng == "act":
                return nc.scalar.copy(out, in_)
            if eng == "dve":
                return nc.vector.tensor_copy(out, in_)
            return nc.gpsimd.tensor_copy(out, in_)

        if STOP == "empty":
            return nc
        S.op("pool", lambda: nc.gpsimd.memset(ones_f[:], 1.0), writes=["ones_f"])
        S.op("pool", lambda: nc.gpsimd.memset(ones_b[:], 1.0), writes=["ones_b"])
        S.op("pool", lambda: nc.gpsimd.affine_select(
            out=ident[:], in_=ones_f[:], pattern=[[1, 128]], compare_op=ALU.is_equal,
            fill=0.0, base=0, channel_multiplier=-1), reads=["ones_f"], writes=["ident"])
        S.op("pool", lambda: nc.gpsimd.affine_select(
            out=sel[:, 0, :], in_=ones_f[0:2, :], pattern=[[0, 128]], compare_op=ALU.is_equal,
            fill=0.0, base=0, channel_multiplier=1), reads=["ones_f"], writes=["sel"])
        S.op("pool", lambda: nc.gpsimd.affine_select(
            out=sel[:, 1, :], in_=ones_f[0:2, :], pattern=[[0, 128]], compare_op=ALU.is_equal,
            fill=0.0, base=-1, channel_multiplier=1), reads=["ones_f"], writes=["sel"])
        S.dma("sp", "hv", hvt[:], hv[:], writes=["hvt"])
        S.op("act", lambda: nc.scalar.activation(out=ones_hv[:], in_=ones_f[:], func=AF.Identity,
                                                 scale=hvt[:]), reads=["ones_f", "hvt"], writes=["ones_hv"])

        def load_cols(src, nrows, dst, key):
            sg = next_stg()
            S.dma("sp", f"g{sg}", STG[sg][0:nrows, 0:128], src, writes=[f"stg:{sg}"])
            t = next_tp()
            S.op("pe", lambda: nc.tensor.transpose(TP[t][:, 0:nrows], STG[sg][0:nrows, 0:128], ident[0:nrows, 0:nrows]),
                 reads=[f"stg:{sg}", "ident"], writes=[f"tp:{t}"])
            S.op("dve", lambda: nc.vector.tensor_copy(dst, TP[t][:, 0:nrows]), reads=[f"tp:{t}"], writes=[key])

        load_cols(ln_g[:], 64, gcol[:], "gcol")
        load_cols(ln_b[:], 64, bcol[:], "bcol")
        load_cols(conv_w[:], 48, cw[:], "cw")
        for s_ in range(2):
            load_cols(sconv[s_], 32, shT[:, s_, :], f"shT:{s_}")
        S.op("dve", lambda: nc.vector.memset(carry[:], 0.0), writes=["carry"])

        rb = AR[0:NH, 0:513]
        vv = AR[0:NH, 576:576 + 768]
        S.dma("sp", "rb", rb, rel_bias[:], writes=ark(0, 513))
        S.op("act", lambda: nc.scalar.activation(out=rb, in_=rb, func=AF.Exp),
             reads=ark(0, 513), writes=ark(0, 513))
        S.op("dve", lambda: nc.vector.tensor_copy(AR[0:NH, 576:576 + 384], AR[0:NH, 512:513].to_broadcast([NH, 384])),
             reads=ark(0, 513), writes=ark(576, 384))
        S.op("dve", lambda: nc.vector.tensor_copy(AR[0:NH, 576 + 384:576 + 768],
                                                  bass.AP(AR, 511, [[ARC, NH], [-1, 384]])),
             reads=ark(0, 513), writes=ark(960, 384))
        S.dma("sp", "eb", ebias[:], vv, reads=ark(576, 768), writes=["ebias"])

        def wload(src, nk):
            slot = cnt["w"] % NSLOT
            cnt["w"] += 1
            S.dma("pool", f"w{slot}", WR[slot][:, 0:nk, :], src, writes=[f"wr:{slot}"])
            return slot

        def wview(wt, nk_total):
            return wt[:].rearrange("(k p) n -> p k n", p=128)

        WV = {
            "w_in": wview(w_in, KC), "w_out": wview(w_out, KC), "w_kv": wview(w_kv, KC),
            "w_q": wview(w_q, KC), "w_o": wview(w_o, KC),
            "w_gu0": wview(w_gu[0], KC), "w_gu1": wview(w_gu[1], KC),
            "w_dn0": wview(w_dn[0], FC), "w_dn1": wview(w_dn[1], FC),
        }

        class Seg:
            def __init__(self, c0, n):
                self.c0, self.n = c0, n

        def tl(c0, n):
            return range(c0 // 128, min(4, (c0 + n - 1) // 128) + 1)

        def x32k(f, c0, n):
            return [f"x32:{f}:{t}" for t in tl(c0, n)]

        def xTk(f, c0, n):
            return [f"xT:{f}:{t}" for t in tl(c0, n)]

        def rk(j, sg):
            return [f"R:{j}:{t}" for t in tl(sg.c0, sg.n)]

        def pieces(sg, g):
            if g == 0:
                return [(sg.c0, sg.n, "p")]
            out = []
            if sg.c0 < 512:
                out.append((sg.c0, min(sg.c0 + sg.n, 512) - sg.c0, "p"))
            if sg.c0 + sg.n > 512:
                c = max(sg.c0, 512)
                out.append((c, sg.c0 + sg.n - c, "s"))
            return out

        def proj(wname, col0, k0, nk, rhs_fn, rhs_keys_fn, segs):
            slot = wload(WV[wname][:, k0:k0 + nk, col0:col0 + 128], nk)
            m = next_mm()
            for k in range(nk):
                for si, sg in enumerate(segs):
                    last = (k == nk - 1) and (si == len(segs) - 1)
                    S.op("pe", lambda: nc.tensor.matmul(
                        MM[m][:, si * 512: si * 512 + sg.n], WR[slot][:, k, :], rhs_fn(k, sg),
                        start=(k == 0), stop=(k == nk - 1)),
                        reads=[f"wr:{slot}"] + rhs_keys_fn(k, sg), writes=[f"mm:{m}:{si}"], signal=last)
            return m

        def mmv(m, si, sg):
            return MM[m][:, si * 512: si * 512 + sg.n]

        def mmp(m, si, sg, pc0, pn):
            o = si * 512 + (pc0 - sg.c0)
            return MM[m][:, o:o + pn]

        def x_rhs(k, sg):
            return xT[:, k, sg.c0:sg.c0 + sg.n]

        def x_keys(k, sg):
            return xTk(k, sg.c0, sg.n)

        def r_rhs(k, sg):
            return R[:, k, sg.c0:sg.c0 + sg.n]

        def r_keys(k, sg):
            return rk(k, sg)

        def load_tiles(tl_list):
            for q in range(4):
                for (src, row0, r, c0) in tl_list:
                    sg = next_stg()
                    S.dma("sp", f"g{sg}", STG[sg][0:r, :], src[row0:row0 + r, q * 512:(q + 1) * 512], writes=[f"stg:{sg}"])
                    t = next_tp()
                    for i in range(4):
                        S.op("pe", lambda: nc.tensor.transpose(TP[t][:, i * 128:i * 128 + r],
                                                               STG[sg][0:r, i * 128:(i + 1) * 128], ident[0:r, 0:r]),
                             reads=[f"stg:{sg}", "ident"], writes=[f"tp:{t}"], signal=(i == 3))
                    tpv = TP[t][:].rearrange("p (i c) -> p i c", i=4)[:, :, 0:r]
                    S.op("dve", lambda: nc.vector.tensor_copy(x32T[:, 4 * q:4 * q + 4, c0:c0 + r], tpv),
                         reads=[f"tp:{t}"], writes=[k_ for i in range(4) for k_ in x32k(4 * q + i, c0, r)])
                    S.op("act", lambda: nc.scalar.copy(xT[:, 4 * q:4 * q + 4, c0:c0 + r], x32T[:, 4 * q:4 * q + 4, c0:c0 + r]),
                         reads=[k_ for i in range(4) for k_ in x32k(4 * q + i, c0, r)],
                         writes=[k_ for i in range(4) for k_ in xTk(4 * q + i, c0, r)])

        def store_tiles(tl_list):
            for q in range(4):
                for (dst, row0, r, c0) in tl_list:
                    t = next_tp()
                    for i in range(4):
                        S.op("pe", lambda: nc.tensor.transpose(TP[t][0:r, i * 128:(i + 1) * 128],
                                                               x32T[:, 4 * q + i, c0:c0 + r], ident[:, :]),
                             reads=x32k(4 * q + i, c0, r) + ["ident"], writes=[f"tp:{t}"], signal=(i == 3))
                    sg = next_stg()
                    e = ev_eng()
                    S.op(e, lambda: copy_on(e, STG[sg][0:r, :], TP[t][0:r, :]), reads=[f"tp:{t}"], writes=[f"stg:{sg}"])
                    S.dma("sp", f"g{sg}", dst[row0:row0 + r, q * 512:(q + 1) * 512], STG[sg][0:r, :],
                          reads=[f"stg:{sg}"], writes=[f"out:{dst.name}"])

        def layer_norm(ln_idx, segs, tiles, write_xT):
            tmpc = [0, 576]
            rowv_c = 1152
            rowv = AR[0:2, rowv_c:rowv_c + GW]

            def stats_a(ti):
                c0, r = tiles[ti]
                for q in range(4):
                    t = next_tp()
                    for i in range(4):
                        S.op("pe", lambda: nc.tensor.transpose(TP[t][0:r, i * 128:(i + 1) * 128],
                                                               x32T[:, 4 * q + i, c0:c0 + r], ident[:, :]),
                             reads=x32k(4 * q + i, c0, r) + ["ident"], writes=[f"tp:{t}"], signal=(i == 3))
                    S.op("dve", lambda: nc.vector.bn_stats(stats[0:r, q, :], TP[t][0:r, :]),
                         reads=[f"tp:{t}"], writes=["stats"])
                S.op("dve", lambda: nc.vector.bn_aggr(mv[0:r, ti, :], stats[0:r, :, :].rearrange("p a b -> p (a b)")),
                     reads=["stats"], writes=[f"mv:{ti}"])
                S.op("act", lambda: nc.scalar.activation(out=std[0:r, ti:ti + 1], in_=mv[0:r, ti, 1:2], func=AF.Sqrt,
                                                         bias=EPS, scale=1.0), reads=[f"mv:{ti}"], writes=[f"std:{ti}"])

            def stats_b(ti):
                c0, r = tiles[ti]
                S.op("dve", lambda: nc.vector.reciprocal(rs2[0:r, ti, 0:1], std[0:r, ti:ti + 1]),
                     reads=[f"std:{ti}"], writes=[f"rs2:{ti}"])
                S.op("dve", lambda: nc.vector.tensor_scalar(out=rs2[0:r, ti, 1:2], in0=mv[0:r, ti, 0:1], scalar1=rs2[0:r, ti, 0:1],
                                                            scalar2=-1.0, op0=ALU.mult, op1=ALU.mult),
                     reads=[f"mv:{ti}", f"rs2:{ti}"], writes=[f"rs2:{ti}"])
                t = next_tp()
                S.op("pe", lambda: nc.tensor.transpose(TP[t][0:2, 0:r], rs2[0:r, ti, 0:2], ident[0:r, 0:r]),
                     reads=[f"rs2:{ti}", "ident"], writes=[f"tp:{t}"])
                S.op("act", lambda: nc.scalar.copy(AR[0:2, rowv_c + c0:rowv_c + c0 + r], TP[t][0:2, 0:r]),
                     reads=[f"tp:{t}"], writes=ark(rowv_c + c0, r))

            stats_a(0)
            for ti in range(len(tiles)):
                if ti + 1 < len(tiles):
                    stats_a(ti + 1)
                stats_b(ti)
            ma, mb = next_mm(), next_mm()
            for si, sg in enumerate(segs):
                S.op("pe", lambda: nc.tensor.matmul(mmv(ma, si, sg), sel[:, 0, :], rowv[:, sg.c0:sg.c0 + sg.n],
                                                    start=True, stop=True),
                     reads=["sel"] + ark(rowv_c, GW), writes=[f"mm:{ma}:{si}"])
                S.op("pe", lambda: nc.tensor.matmul(mmv(mb, si, sg), sel[:, 1, :], rowv[:, sg.c0:sg.c0 + sg.n],
                                                    start=True, stop=True),
                     reads=["sel"] + ark(rowv_c, GW), writes=[f"mm:{mb}:{si}"])
            gi = ln_idx * KC
            merged = (len(segs) == 2 and segs[0].n == segs[1].n and segs[0].c0 + segs[0].n == segs[1].c0)
            for n in range(KC):
                tc = tmpc[n % 2]
                gsc, bsc = gcol[:, gi + n:gi + n + 1], bcol[:, gi + n:gi + n + 1]
                if merged:
                    c0, w, ns = segs[0].c0, segs[0].n, 2 * segs[0].n
                    tmp = AR[:, tc + c0:tc + c0 + ns]
                    zt = x32T[:, n, c0:c0 + ns]
                    tk = ark(tc + c0, ns)
                    v3 = lambda ap: ap.rearrange("p (s c) -> p s c", s=2)
                    pa = MM[ma][:, :].rearrange("p (s c) -> p s c", s=2)[:, :, 0:w]
                    pb = MM[mb][:, :].rearrange("p (s c) -> p s c", s=2)[:, :, 0:w]
                    S.op("dve", lambda: nc.vector.tensor_tensor(out=v3(tmp), in0=v3(zt), in1=pa, op=ALU.mult),
                         reads=x32k(n, c0, ns) + [f"mm:{ma}:0", f"mm:{ma}:1"], writes=tk)
                    S.op("dve", lambda: nc.vector.tensor_tensor(out=v3(tmp), in0=v3(tmp), in1=pb, op=ALU.add),
                         reads=tk + [f"mm:{mb}:0", f"mm:{mb}:1"], writes=tk)
                    pieces_ = [(tmp, zt, tk, c0, ns)]
                else:
                    pieces_ = []
                    for si, sg in enumerate(segs):
                        tmp = AR[:, tc + sg.c0:tc + sg.c0 + sg.n]
                        zt = x32T[:, n, sg.c0:sg.c0 + sg.n]
                        tk = ark(tc + sg.c0, sg.n)
                        S.op("dve", lambda: nc.vector.tensor_tensor(out=tmp, in0=zt, in1=mmv(ma, si, sg), op=ALU.mult),
                             reads=x32k(n, sg.c0, sg.n) + [f"mm:{ma}:{si}"], writes=tk)
                        S.op("dve", lambda: nc.vector.tensor_tensor(out=tmp, in0=tmp, in1=mmv(mb, si, sg), op=ALU.add),
                             reads=tk + [f"mm:{mb}:{si}"], writes=tk)
                        pieces_.append((tmp, zt, tk, sg.c0, sg.n))
                for (tmp, zt, tk, c0, ns) in pieces_:
                    if write_xT:
                        S.op("act", lambda: nc.scalar.activation(out=xT[:, n, c0:c0 + ns], in_=tmp,
                                                                 func=AF.Identity, scale=gsc, bias=bsc),
                             reads=tk + ["gcol", "bcol"], writes=xTk(n, c0, ns))
                    S.op("pool", lambda: nc.gpsimd.tensor_scalar(out=zt, in0=tmp, scalar1=gsc, scalar2=bsc,
                                                                 op0=ALU.mult, op1=ALU.add),
                         reads=tk + ["gcol", "bcol"], writes=x32k(n, c0, ns))

        def resid_evac(m, n, segs, first=True):
            for si, sg in enumerate(segs):
                zt = x32T[:, n, sg.c0:sg.c0 + sg.n]
                if first:
                    S.op("dve", lambda: nc.vector.scalar_tensor_tensor(out=zt, in0=zt, scalar=ALPHA, in1=mmv(m, si, sg),
                                                                       op0=ALU.mult, op1=ALU.add),
                         reads=x32k(n, sg.c0, sg.n) + [f"mm:{m}:{si}"], writes=x32k(n, sg.c0, sg.n))
                else:
                    S.op("dve", lambda: nc.vector.tensor_tensor(out=zt, in0=zt, in1=mmv(m, si, sg), op=ALU.add),
                         reads=x32k(n, sg.c0, sg.n) + [f"mm:{m}:{si}"], writes=x32k(n, sg.c0, sg.n))

        def ffn(layer, segs):
            gu, dn = f"w_gu{layer}", f"w_dn{layer}"
            sgc = [2304, 2304 + 576]
            for hf in range(2):
                for jj in range(FH):
                    j = hf * FH + jj
                    mg = proj(gu, j * 128, 0, KC, x_rhs, x_keys, segs)
                    mu = proj(gu, DFF + j * 128, 0, KC, x_rhs, x_keys, segs)
                    sc = sgc[jj % 2]
                    for si, sg in enumerate(segs):
                        sgt = AR[:, sc + sg.c0:sc + sg.c0 + sg.n]
                        S.op("act", lambda: nc.scalar.activation(out=sgt, in_=mmv(mg, si, sg), func=AF.Silu),
                             reads=[f"mm:{mg}:{si}"], writes=ark(sc + sg.c0, sg.n))
                        S.op("dve", lambda: nc.vector.tensor_tensor(out=R[:, jj, sg.c0:sg.c0 + sg.n], in0=mmv(mu, si, sg),
                                                                    in1=sgt, op=ALU.mult),
                             reads=[f"mm:{mu}:{si}"] + ark(sc + sg.c0, sg.n), writes=rk(jj, sg))
                for n in range(KC):
                    m = proj(dn, n * 128, hf * FH, FH, r_rhs, r_keys, segs)
                    resid_evac(m, n, segs, first=(hf == 0))

        def mixer(g, segs, chsegs, stream):
            cC, cU, cV = 0, 640, 1280

            def uoff(pc0, kind):
                if g == 0:
                    return pc0
                return 2 + pc0 if kind == "p" else 516 + (pc0 - 512)

            def voff(pc0, kind):
                if g == 0:
                    return pc0 - 2
                return pc0 if kind == "p" else 514 + (pc0 - 512)

            for j in range(KC):
                mc = proj("w_in", D + j * 128, 0, KC, x_rhs, x_keys, chsegs)
                mh = proj("w_in", 2 * D + j * 128, 0, KC, x_rhs, x_keys, chsegs)
                mb_ = proj("w_in", j * 128, 0, KC, x_rhs, x_keys, segs)
                if g > 0:
                    S.op("act", lambda: nc.scalar.copy(AR[:, cU:cU + 2], carry[:, :, j]),
                         reads=["carry"], writes=ark(cU, 2))
                    S.op("act", lambda: nc.scalar.copy(AR[:, cU + 514:cU + 516],
                                                       shT[:, stream, :].rearrange("p (r k) -> p r k", r=2)[:, :, j]),
                         reads=[f"shT:{stream}"], writes=ark(cU + 514, 2))
                for si, sg in enumerate(chsegs):
                    for (pc0, pn, kind) in pieces(sg, g):
                        uo = uoff(pc0, kind)
                        S.op("act", lambda: nc.scalar.copy(AR[:, cC + uo:cC + uo + pn], mmp(mc, si, sg, pc0, pn)),
                             reads=[f"mm:{mc}:{si}"], writes=ark(cC + uo, pn))
                        S.op("dve", lambda: nc.vector.tensor_tensor(out=AR[:, cU + uo:cU + uo + pn], in0=mmp(mh, si, sg, pc0, pn),
                                                                    in1=AR[:, cC + uo:cC + uo + pn], op=ALU.mult),
                             reads=[f"mm:{mh}:{si}"] + ark(cC + uo, pn), writes=ark(cU + uo, pn))
                L = 512 if g == 0 else 578
                S.op("act", lambda: nc.scalar.activation(out=AR[:, cV:cV + L], in_=AR[:, cU:cU + L], func=AF.Identity,
                                                         scale=cw[:, j:j + 1]),
                     reads=ark(cU, L + 2) + ["cw"], writes=ark(cV, L))
                for tap in (1, 2):
                    S.op("dve", lambda: nc.vector.scalar_tensor_tensor(
                        out=AR[:, cV:cV + L], in0=AR[:, cU + tap:cU + tap + L], scalar=cw[:, tap * 16 + j:tap * 16 + j + 1],
                        in1=AR[:, cV:cV + L], op0=ALU.mult, op1=ALU.add),
                        reads=ark(cU, L + 2) + ark(cV, L) + ["cw"], writes=ark(cV, L))
                for si, sg in enumerate(segs):
                    for (pc0, pn, kind) in pieces(sg, g):
                        vo = voff(pc0, kind)
                        S.op("dve", lambda: nc.vector.tensor_tensor(out=R[:, j, pc0:pc0 + pn], in0=mmp(mb_, si, sg, pc0, pn),
                                                                    in1=AR[:, cV + vo:cV + vo + pn], op=ALU.mult),
                             reads=[f"mm:{mb_}:{si}"] + ark(cV + vo, pn), writes=[f"R:{j}:{t}" for t in tl(pc0, pn)])
                S.op("act", lambda: nc.scalar.copy(carry[:, :, j], AR[:, cU + 512:cU + 514]),
                     reads=ark(cU + 512, 2), writes=["carry"])
                if g > 0:
                    S.op("act", lambda: nc.scalar.copy(sstate[:, :, j], AR[:, cU + 578:cU + 580]),
                         reads=ark(cU + 578, 2), writes=["sstate"])

        def store_state(src, key, dst):
            t = next_tp()
            S.op("pe", lambda: nc.tensor.transpose(TP[t][0:32, 0:128], src[:].rearrange("p r k -> p (r k)"), ident[:, :]),
                 reads=[key, "ident"], writes=[f"tp:{t}"])
            sg = next_stg()
            S.op("dve", lambda: nc.vector.tensor_copy(STG[sg][0:32, 0:128], TP[t][0:32, 0:128]),
                 reads=[f"tp:{t}"], writes=[f"stg:{sg}"])
            S.dma("sp", f"g{sg}", dst, STG[sg][0:32, 0:128], reads=[f"stg:{sg}"], writes=[f"out:{key}"])

        def kv_proj(g, segs, own, stream):
            cFb = [3456, 4032]
            out_p = (g == 2)
            out_s = (g > 0)
            toff = 2 if g == 0 else 0

            def rows_out(which, h, kind, cF):
                if kind == "p":
                    dstp = k_p if which == "k" else v_p
                    t = next_tp()
                    for i in range(4):
                        S.op("pe", lambda: nc.tensor.transpose(TP[t][:, i * 128:(i + 1) * 128],
                                                               AR[:, cF + i * 128:cF + (i + 1) * 128], ident[:, :]),
                             reads=ark(cF + i * 128, 128) + ["ident"], writes=[f"tp:{t}"], signal=(i == 3))
                    s2 = next_stg()
                    e = ev_eng()
                    S.op(e, lambda: copy_on(e, STG[s2][:, :], TP[t][:, :]), reads=[f"tp:{t}"], writes=[f"stg:{s2}"])
                    S.dma("sp", f"g{s2}", dstp[:].rearrange("(t p) n -> p t n", p=128)[:, :, h * 128:(h + 1) * 128],
                          STG[s2][:, :].rearrange("p (t d) -> p t d", t=4), reads=[f"stg:{s2}"], writes=[f"out:{which}p"])
                else:
                    dsts = k_s if which == "k" else v_s
                    t = next_tp()
                    S.op("pe", lambda: nc.tensor.transpose(TP[t][0:64, 0:128], AR[:, cF + 512:cF + 576], ident[:, :]),
                         reads=ark(cF + 512, 64) + ["ident"], writes=[f"tp:{t}"])
                    s2 = next_stg()
                    e = ev_eng()
                    S.op(e, lambda: copy_on(e, STG[s2][0:64, 0:128], TP[t][0:64, 0:128]),
                         reads=[f"tp:{t}"], writes=[f"stg:{s2}"])
                    S.dma("sp", f"g{s2}", dsts[stream * 64:(stream + 1) * 64, h * 128:(h + 1) * 128], STG[s2][0:64, 0:128],
                          reads=[f"stg:{s2}"], writes=[f"out:{which}s"])

            def k_a(h):
                cF = cFb[h % 2]
                m = proj("w_kv", h * 128, 0, KC, x_rhs, x_keys, segs)
                for si, sg in enumerate(segs):
                    for (pc0, pn, kind) in pieces(sg, g):
                        tk0 = pc0 - toff
                        if kind == "p":
                            dst, key = KT[:, h, own * 512 + tk0:own * 512 + tk0 + pn], f"KT:{own}:{h}"
                        else:
                            dst, key = KTs[:, h, tk0 - 512:tk0 - 512 + pn], f"KTs:{h}"
                        S.op("act", lambda: nc.scalar.copy(dst, mmp(m, si, sg, pc0, pn)), reads=[f"mm:{m}:{si}"], writes=[key])
                        if (kind == "p" and out_p) or (kind == "s" and out_s):
                            S.op("dve", lambda: nc.vector.tensor_copy(AR[:, cF + tk0:cF + tk0 + pn], mmp(m, si, sg, pc0, pn)),
                                 reads=[f"mm:{m}:{si}"], writes=ark(cF + tk0, pn))

            def k_b(h):
                cF = cFb[h % 2]
                if out_p:
                    rows_out("k", h, "p", cF)
                if out_s:
                    rows_out("k", h, "s", cF)

            def v_a(h):
                cF = cFb[h % 2]
                m = proj("w_kv", D + h * 128, 0, KC, x_rhs, x_keys, segs)
                for si, sg in enumerate(segs):
                    for (pc0, pn, kind) in pieces(sg, g):
                        tk0 = pc0 - toff
                        S.op("dve", lambda: nc.vector.tensor_copy(AR[:, cF + tk0:cF + tk0 + pn], mmp(m, si, sg, pc0, pn)),
                             reads=[f"mm:{m}:{si}"], writes=ark(cF + tk0, pn))

            def v_b(h):
                cF = cFb[h % 2]
                t = next_tp()
                for i in range(4):
                    S.op("pe", lambda: nc.tensor.transpose(TP[t][:, i * 128:(i + 1) * 128],
                                                           AR[:, cF + i * 128:cF + (i + 1) * 128], ident[:, :]),
                         reads=ark(cF + i * 128, 128) + ["ident"], writes=[f"tp:{t}"], signal=(i == 3))
                vdst = V[:, own * 4:own * 4 + 4, h * 128:(h + 1) * 128]
                tpv = TP[t][:, :].rearrange("p (t d) -> p t d", t=4)
                vkeys = [f"V:{own * 4 + i}:{h}" for i in range(4)]
                if g == 0:
                    S.op("act", lambda: nc.scalar.activation(out=vdst, in_=tpv, func=AF.Identity, scale=hvt[:]),
                         reads=[f"tp:{t}", "hvt"], writes=vkeys)
                else:
                    S.op("act", lambda: nc.scalar.copy(vdst, tpv), reads=[f"tp:{t}"], writes=vkeys)
                if out_p:
                    s2 = next_stg()
                    S.op("dve", lambda: nc.vector.tensor_copy(STG[s2][:, :], TP[t][:, :]),
                         reads=[f"tp:{t}"], writes=[f"stg:{s2}"])
                    S.dma("sp", f"g{s2}", v_p[:].rearrange("(t p) n -> p t n", p=128)[:, :, h * 128:(h + 1) * 128],
                          STG[s2][:, :].rearrange("p (t d) -> p t d", t=4), reads=[f"stg:{s2}"], writes=["out:vp"])
                if g > 0:
                    t = next_tp()
                    S.op("pe", lambda: nc.tensor.transpose(TP[t][0:64, 0:128], AR[:, cF + 512:cF + 576], ident[:, :]),
                         reads=ark(cF + 512, 64) + ["ident"], writes=[f"tp:{t}"])
                    S.op("act", lambda: nc.scalar.copy(Vs[0:64, h * 128:(h + 1) * 128], TP[t][0:64, 0:128]),
                         reads=[f"tp:{t}"], writes=[f"Vs:{h}"])
                    s2 = next_stg()
                    S.op("dve", lambda: nc.vector.tensor_copy(STG[s2][0:64, 0:128], TP[t][0:64, 0:128]),
                         reads=[f"tp:{t}"], writes=[f"stg:{s2}"])
                    S.dma("sp", f"g{s2}", v_s[stream * 64:(stream + 1) * 64, h * 128:(h + 1) * 128], STG[s2][0:64, 0:128],
                          reads=[f"stg:{s2}"], writes=["out:vs"])

            for fa, fb in ((k_a, k_b), (v_a, v_b)):
                fa(0)
                for h in range(NH):
                    if h + 1 < NH:
                        fa(h + 1)
                    fb(h)

        def attention(g, own, stream):
            prev = 1 - own
            cE = [0, 640]
            cH = [1280, 1920]
            cKS = [2560, 3072]
            cVS = [3584, 4096]
            cRI = [4608, 4608]

            def prologue(h):
                b = h % 2
                S.dma("sp", f"hk{b}", AR[:, cH[b]:cH[b] + 640], bass.AP(ebias, h * 768, [[1, 128], [1, 640]]),
                      reads=["ebias"], writes=ark(cH[b], 640))
                S.op("pool", lambda: nc.gpsimd.memset(AR[0:64, cH[b]:cH[b] + 64], 0.0), writes=ark(cH[b], 64))
                S.op("pool", lambda: nc.gpsimd.memset(AR[64:128, cH[b] + 576:cH[b] + 640], 0.0), writes=ark(cH[b] + 576, 64))
                S.dma("sp", f"ks{b}", AR[:, cKS[b]:cKS[b] + 512].rearrange("p (t d) -> p t d", t=4),
                      ck[stream].rearrange("(t p) n -> p t n", p=128)[:, :, h * 128:(h + 1) * 128],
                      writes=ark(cKS[b], 512))
                S.dma("sp", f"vs{b}", AR[:, cVS[b]:cVS[b] + 512].rearrange("p (t d) -> p t d", t=4),
                      cv[stream].rearrange("(t p) n -> p t n", p=128)[:, :, h * 128:(h + 1) * 128],
                      writes=ark(cVS[b], 512))
                t = next_tp()
                for i in range(4):
                    S.op("pe", lambda: nc.tensor.transpose(TP[t][:, i * 128:(i + 1) * 128],
                                                           AR[:, cKS[b] + i * 128:cKS[b] + (i + 1) * 128], ident[:, :]),
                         reads=ark(cKS[b] + i * 128, 128) + ["ident"], writes=[f"tp:{t}"], signal=(i == 3))
                S.op("act", lambda: nc.scalar.copy(KTc[b][:, :], TP[t][:, :]), reads=[f"tp:{t}"], writes=[f"KTc:{b}"])
                S.op("pool", lambda: nc.gpsimd.tensor_copy(Vc[b][:, :, :], AR[:, cVS[b]:cVS[b] + 512].rearrange("p (t d) -> p t d", t=4)),
                     reads=ark(cVS[b], 512), writes=[f"Vc:{b}"])

            def stage_a(w):
                gi, h, kind, t_ = w
                b, eb = h % 2, gi % 2
                m = next_mm()
                qk = f"R:{h}:{t_}"
                if kind == "p":
                    q_ap = R[:, h, t_ * 128:(t_ + 1) * 128]
                    for mm_ in range(5):
                        c = t_ + mm_
                        half, cc = (prev, c) if c < 4 else (own, c - 4)
                        S.op("pe", lambda: nc.tensor.matmul(MM[m][:, mm_ * 128:(mm_ + 1) * 128],
                                                            KT[:, h, half * 512 + cc * 128: half * 512 + (cc + 1) * 128],
                                                            q_ap, start=True, stop=True),
                             reads=[f"KT:{half}:{h}", qk], writes=[f"mm:{m}:{mm_ // 4}"], signal=(mm_ == 4))
                    S.op("act", lambda: nc.scalar.activation(out=AR[:, cE[eb]:cE[eb] + 640], in_=MM[m][:, 0:640], func=AF.Exp),
                         reads=[f"mm:{m}:0", f"mm:{m}:1"], writes=ark(cE[eb], 640))
                    hrev = bass.AP(AR, cH[b] + 127, [[ARC, 128], [128, 5], [-1, 128]])
                    S.op("dve", lambda: nc.vector.tensor_tensor(
                        out=PT[eb][:, :].rearrange("p (m q) -> p m q", m=5),
                        in0=AR[:, cE[eb]:cE[eb] + 640].rearrange("p (m q) -> p m q", m=5), in1=hrev, op=ALU.mult),
                        reads=ark(cE[eb], 640) + ark(cH[b], 640), writes=[f"PT:{eb}"])
                else:
                    q_ap = R[:, h, 512:576]
                    for mm_ in range(4):
                        S.op("pe", lambda: nc.tensor.matmul(MM[m][:, mm_ * 64:(mm_ + 1) * 64],
                                                            KTc[b][:, mm_ * 128:(mm_ + 1) * 128], q_ap, start=True, stop=True),
                             reads=[f"KTc:{b}", qk], writes=[f"mm:{m}:0"], signal=False)
                    S.op("pe", lambda: nc.tensor.matmul(MM[m][0:64, 256:320], KTs[:, h, :], q_ap, start=True, stop=True),
                         reads=[f"KTs:{h}", qk], writes=[f"mm:{m}:0"])
                    S.op("act", lambda: nc.scalar.activation(out=AR[:, cE[eb]:cE[eb] + 256], in_=MM[m][:, 0:256], func=AF.Exp),
                         reads=[f"mm:{m}:0"], writes=ark(cE[eb], 256))
                    S.op("act", lambda: nc.scalar.activation(out=AR[0:64, cE[eb] + 256:cE[eb] + 320], in_=MM[m][0:64, 256:320], func=AF.Exp),
                         reads=[f"mm:{m}:0"], writes=ark(cE[eb] + 256, 64))
                    hrev = bass.AP(AR, cH[b] + 127, [[ARC, 128], [128, 4], [-1, 64]])
                    S.op("dve", lambda: nc.vector.tensor_tensor(
                        out=PT[eb][:, 0:256].rearrange("p (m q) -> p m q", m=4),
                        in0=AR[:, cE[eb]:cE[eb] + 256].rearrange("p (m q) -> p m q", m=4), in1=hrev, op=ALU.mult),
                        reads=ark(cE[eb], 256) + ark(cH[b], 640), writes=[f"PT:{eb}"])
                    hrev4 = bass.AP(AR, cH[b] + 512 + 127, [[ARC, 64], [-1, 64]])
                    S.op("dve", lambda: nc.vector.tensor_tensor(
                        out=PT[eb][0:64, 256:320], in0=AR[0:64, cE[eb] + 256:cE[eb] + 320], in1=hrev4, op=ALU.mult),
                        reads=ark(cE[eb] + 256, 64) + ark(cH[b], 640), writes=[f"PT:{eb}"])

            def stage_b(w):
                gi, h, kind, t_ = w
                b, eb = h % 2, gi % 2
                qk = f"R:{h}:{t_}"
                NQ = 128 if kind == "p" else 64
                q_ap = R[:, h, t_ * 128:(t_ + 1) * 128] if kind == "p" else R[:, h, 512:576]
                mo = next_mm()
                nchunk = 5
                for pass_ in range(2):
                    for mm_ in range(nchunk):
                        if kind == "p":
                            c = t_ + mm_
                            half, cc = (prev, c) if c < 4 else (own, c - 4)
                            halo = (g == 1 and c < 4)
                            if pass_ == 0:
                                lhs = V[:, half * 4 + cc, h * 128:(h + 1) * 128]
                                lk = [f"V:{half * 4 + cc}:{h}"]
                            else:
                                lhs = ones_hv[:, :] if halo else ones_b[:, :]
                                lk = ["ones_hv" if halo else "ones_b"]
                            rhs = PT[eb][:, mm_ * 128:(mm_ + 1) * 128]
                        else:
                            if mm_ < 4:
                                lhs = Vc[b][:, mm_, :] if pass_ == 0 else ones_b[:, :]
                                lk = [f"Vc:{b}"] if pass_ == 0 else ["ones_b"]
                                rhs = PT[eb][:, mm_ * 64:(mm_ + 1) * 64]
                            else:
                                lhs = Vs[0:64, h * 128:(h + 1) * 128] if pass_ == 0 else ones_b[0:64, :]
                                lk = [f"Vs:{h}"] if pass_ == 0 else ["ones_b"]
                                rhs = PT[eb][0:64, 256:320]
                        S.op("pe", lambda: nc.tensor.matmul(MM[mo][:, pass_ * 128:pass_ * 128 + NQ], lhs, rhs,
                                                            start=(mm_ == 0), stop=(mm_ == nchunk - 1)),
                             reads=lk + [f"PT:{eb}"], writes=[f"mm:{mo}:0"], signal=(pass_ == 1 and mm_ == nchunk - 1))
                ri = AR[:, cRI[eb]:cRI[eb] + NQ]
                S.op("dve", lambda: nc.vector.reciprocal(ri, MM[mo][:, 128:128 + NQ]),
                     reads=[f"mm:{mo}:0"], writes=ark(cRI[eb], NQ))
                S.op("dve", lambda: nc.vector.tensor_tensor(out=q_ap, in0=MM[mo][:, 0:NQ], in1=ri, op=ALU.mult),
                     reads=[f"mm:{mo}:0"] + ark(cRI[eb], NQ), writes=[qk])

            work = []
            for h in range(NH):
                for kind, t_ in [("p", 0), ("p", 1), ("p", 2), ("p", 3), ("s", 4)]:
                    work.append((len(work), h, kind, t_))
            prologue(0)
            stage_a(work[0])
            for i, w in enumerate(work):
                if i + 1 < len(work):
                    nx = work[i + 1]
                    if nx[1] != w[1]:
                        prologue(nx[1])
                    stage_a(nx)
                stage_b(w)

        def _emit_groups():
            for g in range(3):
                if g == 0:
                    segs = [Seg(2, 512)]
                    chsegs = [Seg(0, 257), Seg(257, 257)]
                    tiles = [(2 + 128 * i, 128) for i in range(4)]
                    own, stream = 0, 0
                    load_tiles([(xp, 0, 2, 0)] + [(xp, 2 + 128 * i, 128, 2 + 128 * i) for i in range(4)])
                else:
                    segs = [Seg(0, 288), Seg(288, 288)]
                    chsegs = segs
                    tiles = [(128 * i, 128) for i in range(4)] + [(512, 64)]
                    own, stream = (1, 0) if g == 1 else (0, 1)
                    load_tiles([(xp, 514 + 512 * (g - 1) + 128 * i, 128, 128 * i) for i in range(4)] + [(xs, 64 * stream, 64, 512)])

                _chk(f"g{g}_load")
                mixer(g, segs, chsegs, stream)
                _chk(f"g{g}_mixer")
                if g > 0:
                    store_state(sstate, "sstate", conv_s[stream])
                if g == 2:
                    store_state(carry, "carry", conv_p[:])
                for n in range(KC):
                    m = proj("w_out", n * 128, 0, KC, r_rhs, r_keys, segs)
                    resid_evac(m, n, segs)
                _chk(f"g{g}_wout")
                layer_norm(0, segs, tiles, True)
                _chk(f"g{g}_ln0")
                ffn(0, segs)
                _chk(f"g{g}_ffn0")
                layer_norm(1, segs, tiles, True)
                _chk(f"g{g}_ln1")
                kv_proj(g, segs, own, stream)
                _chk(f"g{g}_kv")
                if g == 0:
                    continue
                for h in range(NH):
                    m = proj("w_q", h * 128, 0, KC, x_rhs, x_keys, segs)
                    for si, sg in enumerate(segs):
                        S.op("act", lambda: nc.scalar.activation(out=R[:, h, sg.c0:sg.c0 + sg.n], in_=mmv(m, si, sg),
                                                                 func=AF.Identity, scale=QSCALE),
                             reads=[f"mm:{m}:{si}"], writes=rk(h, sg))
                _chk(f"g{g}_q")
                attention(g, own, stream)
                _chk(f"g{g}_attn")
                for n in range(KC):
                    m = proj("w_o", n * 128, 0, KC, r_rhs, r_keys, segs)
                    resid_evac(m, n, segs)
                layer_norm(2, segs, tiles, True)
                ffn(1, segs)
                layer_norm(3, segs, tiles, False)
                store_tiles([(y_p, 512 * (g - 1) + 128 * i, 128, 128 * i) for i in range(4)] + [(y_s, 64 * stream, 64, 512)])


        try:
            _chk("init")
            _emit_groups()
        except _Stop:
            pass
        for i_ in range(4):
            S.eng["sp"].wait_ge(S.sem[f"g{i_}"], 16 * S.n[f"g{i_}"])
        build_program.stats = (S.n_ins, S.n_wait, dict(S.n))
    return nc


def _layout_inputs(x_prompt, x_sample, state_conv, cache_k, cache_v, w_in_a, conv_w, w_out_a,
                   w_kv, w_q, w_o, rel_bias, ln_g, ln_b, w_gate_up, w_down):
    f = lambda a: np.ascontiguousarray(np.asarray(a, dtype=np.float32))
    xp_full = f(x_prompt)[0]
    xs_full = f(x_sample).reshape(16 * 64, D)
    sc = f(state_conv)[0]
    ckf = f(cache_k).reshape(16, 512, D)
    cvf = f(cache_v).reshape(16, 512, D)
    shared = {
        "w_in": f(w_in_a)[0], "conv_w": f(conv_w)[0].reshape(48, 128), "w_out": f(w_out_a)[0],
        "w_kv": f(w_kv), "w_q": f(w_q)[0], "w_o": f(w_o)[0], "rel_bias": f(rel_bias)[0],
        "ln_g": f(ln_g).reshape(64, 128), "ln_b": f(ln_b).reshape(64, 128),
        "w_gu0": f(w_gate_up)[0], "w_gu1": f(w_gate_up)[1], "w_dn0": f(w_down)[0], "w_dn1": f(w_down)[1],
    }
    padded = np.concatenate([np.zeros((514, D), np.float32), xp_full], axis=0)
    in_maps = []
    for i in range(NCORES):
        m = dict(shared)
        m["xp"] = np.ascontiguousarray(padded[1024 * i: 1024 * i + 1538])
        m["xs"] = np.ascontiguousarray(xs_full[128 * i: 128 * i + 128])
        m["sconv"] = np.ascontiguousarray(sc[2 * i: 2 * i + 2].reshape(2, 32, 128))
        m["ck"] = np.ascontiguousarray(ckf[2 * i: 2 * i + 2])
        m["cv"] = np.ascontiguousarray(cvf[2 * i: 2 * i + 2])
        m["hv"] = np.full((128, 1), 0.0 if i == 0 else 1.0, np.float32)
        in_maps.append(m)
    return in_maps


def kernel(**inputs):
    in_maps = _layout_inputs(**inputs)
    nc = build_program()
    res = run_bass_kernel_spmd(nc, in_maps, core_ids=list(range(NCORES)))
    r = res.results
    y_prompt = np.concatenate([r[i]["y_p"] for i in range(NCORES)], axis=0).reshape(1, 8192, D)
    y_sample = np.concatenate([r[i]["y_s"] for i in range(NCORES)], axis=0).reshape(16, 64, D)
    conv_prompt = r[NCORES - 1]["conv_p"].reshape(1, 1, 2, D)
    conv_sample = np.concatenate([r[i]["conv_s"].reshape(2, 2, D) for i in range(NCORES)], axis=0).reshape(1, 16, 2, D)
    k_prompt = r[NCORES - 1]["k_p"].reshape(1, 512, NH, 128)
    v_prompt = r[NCORES - 1]["v_p"].reshape(1, 512, NH, 128)
    k_sample = np.concatenate([r[i]["k_s"] for i in range(NCORES)], axis=0).reshape(16, 64, NH, 128)
    v_sample = np.concatenate([r[i]["v_s"] for i in range(NCORES)], axis=0).reshape(16, 64, NH, 128)
    outs = (y_prompt, y_sample, conv_prompt, conv_sample, k_prompt, v_prompt, k_sample, v_sample)
    return tuple(np.ascontiguousarray(o, dtype=np.float32) for o in outs)
```

```python
import contextlib
import numpy as np
import concourse.bass as bass
import concourse.mybir as mybir
from concourse.bass_utils import run_bass_kernel_spmd

F32 = mybir.dt.float32
BF16 = mybir.dt.bfloat16
AF = mybir.ActivationFunctionType
ALU = mybir.AluOpType

D = 2048
KC = 16
DFF = 5632
FC = 44
FH = 22
NH = 16
ALPHA = 4.0 ** 0.25
EPS = 1e-5
QSCALE = 128.0 ** -0.5
NCORES = 8
GW = 576
NSLOT = 4
ARC = 4736
SMALLW = False
STOP = None


class _Stop(Exception):
    pass


def _chk(tag):
    if STOP == tag:
        raise _Stop()


class Sched:
    def __init__(self, nc, sems, dma_channels):
        self.nc = nc
        self.eng = {"pe": nc.tensor, "dve": nc.vector, "act": nc.scalar,
                    "pool": nc.gpsimd, "sp": nc.sync}
        self.sem = dict(sems)
        self.n = {k: 0 for k in self.sem}
        self.sigs = {k: [] for k in self.sem}
        self.waited = {}
        self.last_w = {}
        self.readers = {}
        self.dma_channels = set(dma_channels)
        self.n_wait = 0
        self.n_ins = 0

    def _sem_value(self, e, idx):
        if e in self.dma_channels:
            return 16 * (idx + 1)
        s = self.sigs[e]
        lo, hi = 0, len(s)
        while lo < hi:
            mid = (lo + hi) // 2
            if s[mid] >= idx:
                hi = mid
            else:
                lo = mid + 1
        assert lo < len(s), f"no signalled instr on {e} at/after {idx}"
        return lo + 1

    def _deps(self, reads, writes):
        deps = []
        lw = self.last_w
        for k in reads:
            w = lw.get(k)
            if w is not None:
                deps.append(w)
        for k in writes:
            w = lw.get(k)
            if w is not None:
                deps.append(w)
            r = self.readers.get(k)
            if r:
                deps.extend(r.items())
        return deps

    def _emit_waits(self, on, deps, skip_same=False):
        best = {}
        for e, i in deps:
            if skip_same and e == on:
                continue
            if e not in best or i > best[e]:
                best[e] = i
        for e, i in best.items():
            v = self._sem_value(e, i)
            if self.waited.get((on, e), 0) >= v:
                continue
            self.eng[on].wait_ge(self.sem[e], v)
            self.waited[(on, e)] = v
            self.n_wait += 1

    def _record(self, who, idx, reads, writes):
        for k in reads:
            self.readers.setdefault(k, {})[who] = idx
        for k in writes:
            self.last_w[k] = (who, idx)
            self.readers[k] = {}

    def op(self, on, ins_fn, reads=(), writes=(), signal=True):
        ex = [k for k in reads if k[:3] in ("mm:", "tp:")]
        if ex and on != "pe":
            writes = list(writes) + ex
        self._emit_waits(on, self._deps(reads, writes), skip_same=(on == "pe"))
        ins = ins_fn()
        idx = self.n[on]
        self.n[on] += 1
        self.n_ins += 1
        if signal:
            ins.then_inc(self.sem[on], 1)
            self.sigs[on].append(idx)
        self._record(on, idx, reads, writes)
        return ins

    def dma(self, queue, chan, out, in_, reads=(), writes=()):
        self._emit_waits(queue, self._deps(reads, writes))
        ins = self.eng[queue].dma_start(out=out, in_=in_)
        ins.then_inc(self.sem[chan], 16)
        idx = self.n[chan]
        self.n[chan] += 1
        self.n_ins += 1
        self._record(chan, idx, reads, writes)
        return ins

    def wait_all(self, on, keys):
        self._emit_waits(on, [self.last_w[k] for k in keys if k in self.last_w])


def build_program():
    nc = bass.Bass("TRN2", target_bir_lowering=False)

    def din(name, shape):
        return nc.dram_tensor(name, list(shape), F32, kind="ExternalInput")

    def dout(name, shape):
        return nc.dram_tensor(name, list(shape), F32, kind="ExternalOutput")

    xp = din("xp", [1538, D])
    xs = din("xs", [128, D])
    sconv = din("sconv", [2, 32, 128])
    ck = din("ck", [2, 512, D])
    cv = din("cv", [2, 512, D])
    hv = din("hv", [128, 1])
    if SMALLW:
        _din = din
        din = lambda name, shape: _din(name, [128, 128] if name.startswith("w_") else shape)
    w_in = din("w_in", [D, 3 * D])
    conv_w = din("conv_w", [48, 128])
    w_out = din("w_out", [D, D])
    w_kv = din("w_kv", [D, 2 * D])
    w_q = din("w_q", [D, D])
    w_o = din("w_o", [D, D])
    rel_bias = din("rel_bias", [NH, 513])
    ln_g = din("ln_g", [64, 128])
    ln_b = din("ln_b", [64, 128])
    w_gu = [din("w_gu0", [D, 2 * DFF]), din("w_gu1", [D, 2 * DFF])]
    w_dn = [din("w_dn0", [DFF, D]), din("w_dn1", [DFF, D])]

    y_p = dout("y_p", [1024, D])
    y_s = dout("y_s", [128, D])
    conv_p = dout("conv_p", [32, 128])
    conv_s = dout("conv_s", [2, 32, 128])
    k_p = dout("k_p", [512, D])
    v_p = dout("v_p", [512, D])
    k_s = dout("k_s", [128, D])
    v_s = dout("v_s", [128, D])
    ebias = nc.dram_tensor("ebias", [NH, 768], F32)

    dma_chans = ([f"w{i}" for i in range(NSLOT)] + [f"g{i}" for i in range(4)]
                 + ["hk0", "hk1", "ks0", "ks1", "vs0", "vs1", "hv", "rb", "eb"])
    sem_names = ["pe", "dve", "act", "pool", "sp"] + dma_chans
    with contextlib.ExitStack() as st:
        sems = {n: st.enter_context(nc.semaphore(n)) for n in sem_names}
        S = Sched(nc, sems, dma_chans)

        def sb(name, shape, dt):
            return st.enter_context(nc.sbuf_tensor(name, list(shape), dt))

        def ps(name, shape, dt=F32):
            return st.enter_context(nc.psum_tensor(name, list(shape), dt))

        x32T = sb("x32T", [128, KC, GW], F32)
        xT = sb("xT", [128, KC, GW], BF16)
        KT = sb("KT", [128, NH, 1024], BF16)
        KTs = sb("KTs", [128, NH, 64], BF16)
        V = sb("V", [128, 8, D], BF16)
        Vs = sb("Vs", [128, D], BF16)
        R = sb("R", [128, FH, GW], BF16)
        WR = [sb(f"WR{i}", [128, FH, 128], BF16) for i in range(NSLOT)]
        STG = [sb(f"STG{i}", [128, 512], F32) for i in range(4)]
        AR = sb("AR", [128, ARC], F32)
        PT = [sb(f"PT{i}", [128, 640], BF16) for i in range(2)]
        KTc = [sb(f"KTc{i}", [128, 512], BF16) for i in range(2)]
        Vc = [sb(f"Vc{i}", [128, 4, 128], BF16) for i in range(2)]
        ident = sb("ident", [128, 128], F32)
        ones_f = sb("ones_f", [128, 128], F32)
        ones_b = sb("ones_b", [128, 128], BF16)
        ones_hv = sb("ones_hv", [128, 128], BF16)
        sel = sb("sel", [2, 2, 128], F32)
        stats = sb("stats", [128, 4, 6], F32)
        mv = sb("mv", [128, 5, 2], F32)
        rs2 = sb("rs2", [128, 5, 2], F32)
        std = sb("std", [128, 5], F32)
        gcol = sb("gcol", [128, 64], F32)
        bcol = sb("bcol", [128, 64], F32)
        cw = sb("cw", [128, 48], F32)
        carry = sb("carry", [128, 2, KC], F32)
        shT = sb("shT", [128, 2, 32], F32)
        sstate = sb("sstate", [128, 2, KC], F32)
        hvt = sb("hvt", [128, 1], F32)

        MM = [ps(f"MM{i}", [128, 1024]) for i in range(3)]
        TP = [ps(f"TP{i}", [128, 512]) for i in range(2)]

        def ar(c0, n):
            return AR[:, c0:c0 + n]

        def ark(c0, n):
            return [f"ar:{p}" for p in range(c0 // 64, (c0 + n - 1) // 64 + 1)]

        cnt = {"mm": 0, "tp": 0, "stg": 0, "w": 0, "ev": 0}

        def next_mm():
            s = cnt["mm"] % 3
            cnt["mm"] += 1
            return s

        def next_tp():
            s = cnt["tp"] % 2
            cnt["tp"] += 1
            return s

        def next_stg():
            s = cnt["stg"] % 4
            cnt["stg"] += 1
            return s

        def ev_eng():
            cnt["ev"] += 1
            return "act" if cnt["ev"] % 2 else "dve"

        def copy_on(eng, out, in_):
            if eng == "act":
                return nc.scalar.copy(out, in_)
            if eng == "dve":
                return nc.vector.tensor_copy(out, in_)
            return nc.gpsimd.tensor_copy(out, in_)

        if STOP == "empty":
            return nc
        S.op("pool", lambda: nc.gpsimd.memset(ones_f[:], 1.0), writes=["ones_f"])
        S.op("pool", lambda: nc.gpsimd.memset(ones_b[:], 1.0), writes=["ones_b"])
        S.op("pool", lambda: nc.gpsimd.affine_select(
            out=ident[:], in_=ones_f[:], pattern=[[1, 128]], compare_op=ALU.is_equal,
            fill=0.0, base=0, channel_multiplier=-1), reads=["ones_f"], writes=["ident"])
        S.op("pool", lambda: nc.gpsimd.affine_select(
            out=sel[:, 0, :], in_=ones_f[0:2, :], pattern=[[0, 128]], compare_op=ALU.is_equal,
            fill=0.0, base=0, channel_multiplier=1), reads=["ones_f"], writes=["sel"])
        S.op("pool", lambda: nc.gpsimd.affine_select(
            out=sel[:, 1, :], in_=ones_f[0:2, :], pattern=[[0, 128]], compare_op=ALU.is_equal,
            fill=0.0, base=-1, channel_multiplier=1), reads=["ones_f"], writes=["sel"])
        S.dma("sp", "hv", hvt[:], hv[:], writes=["hvt"])
        S.op("act", lambda: nc.scalar.activation(out=ones_hv[:], in_=ones_f[:], func=AF.Identity,
                                                 scale=hvt[:]), reads=["ones_f", "hvt"], writes=["ones_hv"])

        def load_cols(src, nrows, dst, key):
            sg = next_stg()
            S.dma("sp", f"g{sg}", STG[sg][0:nrows, 0:128], src, writes=[f"stg:{sg}"])
            t = next_tp()
            S.op("pe", lambda: nc.tensor.transpose(TP[t][:, 0:nrows], STG[sg][0:nrows, 0:128], ident[0:nrows, 0:nrows]),
                 reads=[f"stg:{sg}", "ident"], writes=[f"tp:{t}"])
            S.op("dve", lambda: nc.vector.tensor_copy(dst, TP[t][:, 0:nrows]), reads=[f"tp:{t}"], writes=[key])

        load_cols(ln_g[:], 64, gcol[:], "gcol")
        load_cols(ln_b[:], 64, bcol[:], "bcol")
        load_cols(conv_w[:], 48, cw[:], "cw")
        for s_ in range(2):
            load_cols(sconv[s_], 32, shT[:, s_, :], f"shT:{s_}")
        S.op("dve", lambda: nc.vector.memset(carry[:], 0.0), writes=["carry"])

        rb = AR[0:NH, 0:513]
        vv = AR[0:NH, 576:576 + 768]
        S.dma("sp", "rb", rb, rel_bias[:], writes=ark(0, 513))
        S.op("act", lambda: nc.scalar.activation(out=rb, in_=rb, func=AF.Exp),
             reads=ark(0, 513), writes=ark(0, 513))
        S.op("dve", lambda: nc.vector.tensor_copy(AR[0:NH, 576:576 + 384], AR[0:NH, 512:513].to_broadcast([NH, 384])),
             reads=ark(0, 513), writes=ark(576, 384))
        S.op("dve", lambda: nc.vector.tensor_copy(AR[0:NH, 576 + 384:576 + 768],
                                                  bass.AP(AR, 511, [[ARC, NH], [-1, 384]])),
             reads=ark(0, 513), writes=ark(960, 384))
        S.dma("sp", "eb", ebias[:], vv, reads=ark(576, 768), writes=["ebias"])

        def wload(src, nk):
            slot = cnt["w"] % NSLOT
            cnt["w"] += 1
            S.dma("pool", f"w{slot}", WR[slot][:, 0:nk, :], src, writes=[f"wr:{slot}"])
            return slot

        def wview(wt, nk_total):
            return wt[:].rearrange("(k p) n -> p k n", p=128)

        WV = {
            "w_in": wview(w_in, KC), "w_out": wview(w_out, KC), "w_kv": wview(w_kv, KC),
            "w_q": wview(w_q, KC), "w_o": wview(w_o, KC),
            "w_gu0": wview(w_gu[0], KC), "w_gu1": wview(w_gu[1], KC),
            "w_dn0": wview(w_dn[0], FC), "w_dn1": wview(w_dn[1], FC),
        }

        class Seg:
            def __init__(self, c0, n):
                self.c0, self.n = c0, n

        def tl(c0, n):
            return range(c0 // 128, min(4, (c0 + n - 1) // 128) + 1)

        def x32k(f, c0, n):
            return [f"x32:{f}:{t}" for t in tl(c0, n)]

        def xTk(f, c0, n):
            return [f"xT:{f}:{t}" for t in tl(c0, n)]

        def rk(j, sg):
            return [f"R:{j}:{t}" for t in tl(sg.c0, sg.n)]

        def pieces(sg, g):
            if g == 0:
                return [(sg.c0, sg.n, "p")]
            out = []
            if sg.c0 < 512:
                out.append((sg.c0, min(sg.c0 + sg.n, 512) - sg.c0, "p"))
            if sg.c0 + sg.n > 512:
                c = max(sg.c0, 512)
                out.append((c, sg.c0 + sg.n - c, "s"))
            return out

        def proj(wname, col0, k0, nk, rhs_fn, rhs_keys_fn, segs):
            slot = wload(WV[wname][:, k0:k0 + nk, col0:col0 + 128], nk)
            m = next_mm()
            for k in range(nk):
                for si, sg in enumerate(segs):
                    last = (k == nk - 1) and (si == len(segs) - 1)
                    S.op("pe", lambda: nc.tensor.matmul(
                        MM[m][:, si * 512: si * 512 + sg.n], WR[slot][:, k, :], rhs_fn(k, sg),
                        start=(k == 0), stop=(k == nk - 1)),
                        reads=[f"wr:{slot}"] + rhs_keys_fn(k, sg), writes=[f"mm:{m}:{si}"], signal=last)
            return m

        def mmv(m, si, sg):
            return MM[m][:, si * 512: si * 512 + sg.n]

        def mmp(m, si, sg, pc0, pn):
            o = si * 512 + (pc0 - sg.c0)
            return MM[m][:, o:o + pn]

        def x_rhs(k, sg):
            return xT[:, k, sg.c0:sg.c0 + sg.n]

        def x_keys(k, sg):
            return xTk(k, sg.c0, sg.n)

        def r_rhs(k, sg):
            return R[:, k, sg.c0:sg.c0 + sg.n]

        def r_keys(k, sg):
            return rk(k, sg)

        def load_x(src, row0, r, c0):
            for q in range(4):
                sg = next_stg()
                S.dma("sp", f"g{sg}", STG[sg][0:r, :], src[row0:row0 + r, q * 512:(q + 1) * 512], writes=[f"stg:{sg}"])
                t = next_tp()
                for i in range(4):
                    S.op("pe", lambda: nc.tensor.transpose(TP[t][:, i * 128:i * 128 + r],
                                                           STG[sg][0:r, i * 128:(i + 1) * 128], ident[0:r, 0:r]),
                         reads=[f"stg:{sg}", "ident"], writes=[f"tp:{t}"], signal=(i == 3))
                tpv = TP[t][:].rearrange("p (i c) -> p i c", i=4)[:, :, 0:r]
                S.op("dve", lambda: nc.vector.tensor_copy(x32T[:, 4 * q:4 * q + 4, c0:c0 + r], tpv),
                     reads=[f"tp:{t}"], writes=[k_ for i in range(4) for k_ in x32k(4 * q + i, c0, r)])
                S.op("act", lambda: nc.scalar.copy(xT[:, 4 * q:4 * q + 4, c0:c0 + r], x32T[:, 4 * q:4 * q + 4, c0:c0 + r]),
                     reads=[k_ for i in range(4) for k_ in x32k(4 * q + i, c0, r)],
                     writes=[k_ for i in range(4) for k_ in xTk(4 * q + i, c0, r)])

        def store_tiles(tl_list):
            for q in range(4):
                for (dst, row0, r, c0) in tl_list:
                    t = next_tp()
                    for i in range(4):
                        S.op("pe", lambda: nc.tensor.transpose(TP[t][0:r, i * 128:(i + 1) * 128],
                                                               x32T[:, 4 * q + i, c0:c0 + r], ident[:, :]),
                             reads=x32k(4 * q + i, c0, r) + ["ident"], writes=[f"tp:{t}"], signal=(i == 3))
                    sg = next_stg()
                    e = ev_eng()
                    S.op(e, lambda: copy_on(e, STG[sg][0:r, :], TP[t][0:r, :]), reads=[f"tp:{t}"], writes=[f"stg:{sg}"])
                    S.dma("sp", f"g{sg}", dst[row0:row0 + r, q * 512:(q + 1) * 512], STG[sg][0:r, :],
                          reads=[f"stg:{sg}"], writes=[f"out:{dst.name}"])

        def layer_norm(ln_idx, segs, tiles, write_xT):
            tmpc = [0, 576]
            rowv_c = 1152
            rowv = AR[0:2, rowv_c:rowv_c + GW]

            def stats_a(ti):
                c0, r = tiles[ti]
                for q in range(4):
                    t = next_tp()
                    for i in range(4):
                        S.op("pe", lambda: nc.tensor.transpose(TP[t][0:r, i * 128:(i + 1) * 128],
                                                               x32T[:, 4 * q + i, c0:c0 + r], ident[:, :]),
                             reads=x32k(4 * q + i, c0, r) + ["ident"], writes=[f"tp:{t}"], signal=(i == 3))
                    S.op("dve", lambda: nc.vector.bn_stats(stats[0:r, q, :], TP[t][0:r, :]),
                         reads=[f"tp:{t}"], writes=["stats"])
                S.op("dve", lambda: nc.vector.bn_aggr(mv[0:r, ti, :], stats[0:r, :, :].rearrange("p a b -> p (a b)")),
                     reads=["stats"], writes=[f"mv:{ti}"])
                S.op("act", lambda: nc.scalar.activation(out=std[0:r, ti:ti + 1], in_=mv[0:r, ti, 1:2], func=AF.Sqrt,
                                                         bias=EPS, scale=1.0), reads=[f"mv:{ti}"], writes=[f"std:{ti}"])

            def stats_b(ti):
                c0, r = tiles[ti]
                S.op("dve", lambda: nc.vector.reciprocal(rs2[0:r, ti, 0:1], std[0:r, ti:ti + 1]),
                     reads=[f"std:{ti}"], writes=[f"rs2:{ti}"])
                S.op("dve", lambda: nc.vector.tensor_scalar(out=rs2[0:r, ti, 1:2], in0=mv[0:r, ti, 0:1], scalar1=rs2[0:r, ti, 0:1],
                                                            scalar2=-1.0, op0=ALU.mult, op1=ALU.mult),
                     reads=[f"mv:{ti}", f"rs2:{ti}"], writes=[f"rs2:{ti}"])
                t = next_tp()
                S.op("pe", lambda: nc.tensor.transpose(TP[t][0:2, 0:r], rs2[0:r, ti, 0:2], ident[0:r, 0:r]),
                     reads=[f"rs2:{ti}", "ident"], writes=[f"tp:{t}"])
                S.op("act", lambda: nc.scalar.copy(AR[0:2, rowv_c + c0:rowv_c + c0 + r], TP[t][0:2, 0:r]),
                     reads=[f"tp:{t}"], writes=ark(rowv_c + c0, r))

            stats_a(0)
            for ti in range(len(tiles)):
                if ti + 1 < len(tiles):
                    stats_a(ti + 1)
                stats_b(ti)
            ma, mb = next_mm(), next_mm()
            for si, sg in enumerate(segs):
                S.op("pe", lambda: nc.tensor.matmul(mmv(ma, si, sg), sel[:, 0, :], rowv[:, sg.c0:sg.c0 + sg.n],
                                                    start=True, stop=True),
                     reads=["sel"] + ark(rowv_c, GW), writes=[f"mm:{ma}:{si}"])
                S.op("pe", lambda: nc.tensor.matmul(mmv(mb, si, sg), sel[:, 1, :], rowv[:, sg.c0:sg.c0 + sg.n],
                                                    start=True, stop=True),
                     reads=["sel"] + ark(rowv_c, GW), writes=[f"mm:{mb}:{si}"])
            gi = ln_idx * KC
            merged = (len(segs) == 2 and segs[0].n == segs[1].n and segs[0].c0 + segs[0].n == segs[1].c0)
            for n in range(KC):
                tc = tmpc[n % 2]
                gsc, bsc = gcol[:, gi + n:gi + n + 1], bcol[:, gi + n:gi + n + 1]
                if merged:
                    c0, w, ns = segs[0].c0, segs[0].n, 2 * segs[0].n
                    tmp = AR[:, tc + c0:tc + c0 + ns]
                    zt = x32T[:, n, c0:c0 + ns]
                    tk = ark(tc + c0, ns)
                    v3 = lambda ap: ap.rearrange("p (s c) -> p s c", s=2)
                    pa = MM[ma][:, :].rearrange("p (s c) -> p s c", s=2)[:, :, 0:w]
                    pb = MM[mb][:, :].rearrange("p (s c) -> p s c", s=2)[:, :, 0:w]
                    S.op("dve", lambda: nc.vector.tensor_tensor(out=v3(tmp), in0=v3(zt), in1=pa, op=ALU.mult),
                         reads=x32k(n, c0, ns) + [f"mm:{ma}:0", f"mm:{ma}:1"], writes=tk)
                    S.op("dve", lambda: nc.vector.tensor_tensor(out=v3(tmp), in0=v3(tmp), in1=pb, op=ALU.add),
                         reads=tk + [f"mm:{mb}:0", f"mm:{mb}:1"], writes=tk)
                    pieces_ = [(tmp, zt, tk, c0, ns)]
                else:
                    pieces_ = []
                    for si, sg in enumerate(segs):
                        tmp = AR[:, tc + sg.c0:tc + sg.c0 + sg.n]
                        zt = x32T[:, n, sg.c0:sg.c0 + sg.n]
                        tk = ark(tc + sg.c0, sg.n)
                        S.op("dve", lambda: nc.vector.tensor_tensor(out=tmp, in0=zt, in1=mmv(ma, si, sg), op=ALU.mult),
                             reads=x32k(n, sg.c0, sg.n) + [f"mm:{ma}:{si}"], writes=tk)
                        S.op("dve", lambda: nc.vector.tensor_tensor(out=tmp, in0=tmp, in1=mmv(mb, si, sg), op=ALU.add),
                             reads=tk + [f"mm:{mb}:{si}"], writes=tk)
                        pieces_.append((tmp, zt, tk, sg.c0, sg.n))
                for (tmp, zt, tk, c0, ns) in pieces_:
                    if write_xT:
                        S.op("act", lambda: nc.scalar.activation(out=xT[:, n, c0:c0 + ns], in_=tmp,
                                                                 func=AF.Identity, scale=gsc, bias=bsc),
                             reads=tk + ["gcol", "bcol"], writes=xTk(n, c0, ns))
                    S.op("pool", lambda: nc.gpsimd.tensor_scalar(out=zt, in0=tmp, scalar1=gsc, scalar2=bsc,
                                                                 op0=ALU.mult, op1=ALU.add),
                         reads=tk + ["gcol", "bcol"], writes=x32k(n, c0, ns))

        def resid_evac(m, n, segs, first=True):
            for si, sg in enumerate(segs):
                zt = x32T[:, n, sg.c0:sg.c0 + sg.n]
                if first:
                    S.op("dve", lambda: nc.vector.scalar_tensor_tensor(out=zt, in0=zt, scalar=ALPHA, in1=mmv(m, si, sg),
                                                                       op0=ALU.mult, op1=ALU.add),
                         reads=x32k(n, sg.c0, sg.n) + [f"mm:{m}:{si}"], writes=x32k(n, sg.c0, sg.n))
                else:
                    S.op("dve", lambda: nc.vector.tensor_tensor(out=zt, in0=zt, in1=mmv(m, si, sg), op=ALU.add),
                         reads=x32k(n, sg.c0, sg.n) + [f"mm:{m}:{si}"], writes=x32k(n, sg.c0, sg.n))

        def ffn(layer, segs):
            gu, dn = f"w_gu{layer}", f"w_dn{layer}"
            sgc = [2304, 2304 + 576]
            for hf in range(2):
                for jj in range(FH):
                    j = hf * FH + jj
                    mg = proj(gu, j * 128, 0, KC, x_rhs, x_keys, segs)
                    mu = proj(gu, DFF + j * 128, 0, KC, x_rhs, x_keys, segs)
                    sc = sgc[jj % 2]
                    for si, sg in enumerate(segs):
                        sgt = AR[:, sc + sg.c0:sc + sg.c0 + sg.n]
                        S.op("act", lambda: nc.scalar.activation(out=sgt, in_=mmv(mg, si, sg), func=AF.Silu),
                             reads=[f"mm:{mg}:{si}"], writes=ark(sc + sg.c0, sg.n))
                        S.op("dve", lambda: nc.vector.tensor_tensor(out=R[:, jj, sg.c0:sg.c0 + sg.n], in0=mmv(mu, si, sg),
                                                                    in1=sgt, op=ALU.mult),
                             reads=[f"mm:{mu}:{si}"] + ark(sc + sg.c0, sg.n), writes=rk(jj, sg))
                for n in range(KC):
                    m = proj(dn, n * 128, hf * FH, FH, r_rhs, r_keys, segs)
                    resid_evac(m, n, segs, first=(hf == 0))

        def mixer(g, segs, chsegs, stream):
            cC, cU, cV = 0, 640, 1280

            def uoff(pc0, kind):
                if g == 0:
                    return pc0
                return 2 + pc0 if kind == "p" else 516 + (pc0 - 512)

            def voff(pc0, kind):
                if g == 0:
                    return pc0 - 2
                return pc0 if kind == "p" else 514 + (pc0 - 512)

            for j in range(KC):
                mc = proj("w_in", D + j * 128, 0, KC, x_rhs, x_keys, chsegs)
                mh = proj("w_in", 2 * D + j * 128, 0, KC, x_rhs, x_keys, chsegs)
                mb_ = proj("w_in", j * 128, 0, KC, x_rhs, x_keys, segs)
                if g > 0:
                    S.op("act", lambda: nc.scalar.copy(AR[:, cU:cU + 2], carry[:, :, j]),
                         reads=["carry"], writes=ark(cU, 2))
                    S.op("act", lambda: nc.scalar.copy(AR[:, cU + 514:cU + 516],
                                                       shT[:, stream, :].rearrange("p (r k) -> p r k", r=2)[:, :, j]),
                         reads=[f"shT:{stream}"], writes=ark(cU + 514, 2))
                for si, sg in enumerate(chsegs):
                    for (pc0, pn, kind) in pieces(sg, g):
                        uo = uoff(pc0, kind)
                        S.op("act", lambda: nc.scalar.copy(AR[:, cC + uo:cC + uo + pn], mmp(mc, si, sg, pc0, pn)),
                             reads=[f"mm:{mc}:{si}"], writes=ark(cC + uo, pn))
                        S.op("dve", lambda: nc.vector.tensor_tensor(out=AR[:, cU + uo:cU + uo + pn], in0=mmp(mh, si, sg, pc0, pn),
                                                                    in1=AR[:, cC + uo:cC + uo + pn], op=ALU.mult),
                             reads=[f"mm:{mh}:{si}"] + ark(cC + uo, pn), writes=ark(cU + uo, pn))
                L = 512 if g == 0 else 578
                S.op("act", lambda: nc.scalar.activation(out=AR[:, cV:cV + L], in_=AR[:, cU:cU + L], func=AF.Identity,
                                                         scale=cw[:, j:j + 1]),
                     reads=ark(cU, L + 2) + ["cw"], writes=ark(cV, L))
                for tap in (1, 2):
                    S.op("dve", lambda: nc.vector.scalar_tensor_tensor(
                        out=AR[:, cV:cV + L], in0=AR[:, cU + tap:cU + tap + L], scalar=cw[:, tap * 16 + j:tap * 16 + j + 1],
                        in1=AR[:, cV:cV + L], op0=ALU.mult, op1=ALU.add),
                        reads=ark(cU, L + 2) + ark(cV, L) + ["cw"], writes=ark(cV, L))
                for si, sg in enumerate(segs):
                    for (pc0, pn, kind) in pieces(sg, g):
                        vo = voff(pc0, kind)
                        S.op("dve", lambda: nc.vector.tensor_tensor(out=R[:, j, pc0:pc0 + pn], in0=mmp(mb_, si, sg, pc0, pn),
                                                                    in1=AR[:, cV + vo:cV + vo + pn], op=ALU.mult),
                             reads=[f"mm:{mb_}:{si}"] + ark(cV + vo, pn), writes=[f"R:{j}:{t}" for t in tl(pc0, pn)])
                S.op("act", lambda: nc.scalar.copy(carry[:, :, j], AR[:, cU + 512:cU + 514]),
                     reads=ark(cU + 512, 2), writes=["carry"])
                if g > 0:
                    S.op("act", lambda: nc.scalar.copy(sstate[:, :, j], AR[:, cU + 578:cU + 580]),
                         reads=ark(cU + 578, 2), writes=["sstate"])

        def store_state(src, key, dst):
            t = next_tp()
            S.op("pe", lambda: nc.tensor.transpose(TP[t][0:32, 0:128], src[:].rearrange("p r k -> p (r k)"), ident[:, :]),
                 reads=[key, "ident"], writes=[f"tp:{t}"])
            sg = next_stg()
            S.op("dve", lambda: nc.vector.tensor_copy(STG[sg][0:32, 0:128], TP[t][0:32, 0:128]),
                 reads=[f"tp:{t}"], writes=[f"stg:{sg}"])
            S.dma("sp", f"g{sg}", dst, STG[sg][0:32, 0:128], reads=[f"stg:{sg}"], writes=[f"out:{key}"])

        def kv_proj(g, segs, own, stream):
            cFb = [3456, 4032]
            out_p = (g == 2)
            out_s = (g > 0)
            toff = 2 if g == 0 else 0

            def rows_out(which, h, kind, cF):
                if kind == "p":
                    dstp = k_p if which == "k" else v_p
                    t = next_tp()
                    for i in range(4):
                        S.op("pe", lambda: nc.tensor.transpose(TP[t][:, i * 128:(i + 1) * 128],
                                                               AR[:, cF + i * 128:cF + (i + 1) * 128], ident[:, :]),
                             reads=ark(cF + i * 128, 128) + ["ident"], writes=[f"tp:{t}"], signal=(i == 3))
                    s2 = next_stg()
                    e = ev_eng()
                    S.op(e, lambda: copy_on(e, STG[s2][:, :], TP[t][:, :]), reads=[f"tp:{t}"], writes=[f"stg:{s2}"])
                    S.dma("sp", f"g{s2}", dstp[:].rearrange("(t p) n -> p t n", p=128)[:, :, h * 128:(h + 1) * 128],
                          STG[s2][:, :].rearrange("p (t d) -> p t d", t=4), reads=[f"stg:{s2}"], writes=[f"out:{which}p"])
                else:
                    dsts = k_s if which == "k" else v_s
                    t = next_tp()
                    S.op("pe", lambda: nc.tensor.transpose(TP[t][0:64, 0:128], AR[:, cF + 512:cF + 576], ident[:, :]),
                         reads=ark(cF + 512, 64) + ["ident"], writes=[f"tp:{t}"])
                    s2 = next_stg()
                    e = ev_eng()
                    S.op(e, lambda: copy_on(e, STG[s2][0:64, 0:128], TP[t][0:64, 0:128]),
                         reads=[f"tp:{t}"], writes=[f"stg:{s2}"])
                    S.dma("sp", f"g{s2}", dsts[stream * 64:(stream + 1) * 64, h * 128:(h + 1) * 128], STG[s2][0:64, 0:128],
                          reads=[f"stg:{s2}"], writes=[f"out:{which}s"])

            def k_a(h):
                cF = cFb[h % 2]
                m = proj("w_kv", h * 128, 0, KC, x_rhs, x_keys, segs)
                for si, sg in enumerate(segs):
                    for (pc0, pn, kind) in pieces(sg, g):
                        tk0 = pc0 - toff
                        if kind == "p":
                            dst, key = KT[:, h, own * 512 + tk0:own * 512 + tk0 + pn], f"KT:{own}:{h}"
                        else:
                            dst, key = KTs[:, h, tk0 - 512:tk0 - 512 + pn], f"KTs:{h}"
                        S.op("act", lambda: nc.scalar.copy(dst, mmp(m, si, sg, pc0, pn)), reads=[f"mm:{m}:{si}"], writes=[key])
                        if (kind == "p" and out_p) or (kind == "s" and out_s):
                            S.op("dve", lambda: nc.vector.tensor_copy(AR[:, cF + tk0:cF + tk0 + pn], mmp(m, si, sg, pc0, pn)),
                                 reads=[f"mm:{m}:{si}"], writes=ark(cF + tk0, pn))

            def k_b(h):
                cF = cFb[h % 2]
                if out_p:
                    rows_out("k", h, "p", cF)
                if out_s:
                    rows_out("k", h, "s", cF)

            def v_a(h):
                cF = cFb[h % 2]
                m = proj("w_kv", D + h * 128, 0, KC, x_rhs, x_keys, segs)
                for si, sg in enumerate(segs):
                    for (pc0, pn, kind) in pieces(sg, g):
                        tk0 = pc0 - toff
                        S.op("dve", lambda: nc.vector.tensor_copy(AR[:, cF + tk0:cF + tk0 + pn], mmp(m, si, sg, pc0, pn)),
                             reads=[f"mm:{m}:{si}"], writes=ark(cF + tk0, pn))

            def v_b(h):
                cF = cFb[h % 2]
                t = next_tp()
                for i in range(4):
                    S.op("pe", lambda: nc.tensor.transpose(TP[t][:, i * 128:(i + 1) * 128],
                                                           AR[:, cF + i * 128:cF + (i + 1) * 128], ident[:, :]),
                         reads=ark(cF + i * 128, 128) + ["ident"], writes=[f"tp:{t}"], signal=(i == 3))
                vdst = V[:, own * 4:own * 4 + 4, h * 128:(h + 1) * 128]
                tpv = TP[t][:, :].rearrange("p (t d) -> p t d", t=4)
                vkeys = [f"V:{own * 4 + i}:{h}" for i in range(4)]
                if g == 0:
                    S.op("act", lambda: nc.scalar.activation(out=vdst, in_=tpv, func=AF.Identity, scale=hvt[:]),
                         reads=[f"tp:{t}", "hvt"], writes=vkeys)
                else:
                    S.op("act", lambda: nc.scalar.copy(vdst, tpv), reads=[f"tp:{t}"], writes=vkeys)
                if out_p:
                    s2 = next_stg()
                    S.op("dve", lambda: nc.vector.tensor_copy(STG[s2][:, :], TP[t][:, :]),
                         reads=[f"tp:{t}"], writes=[f"stg:{s2}"])
                    S.dma("sp", f"g{s2}", v_p[:].rearrange("(t p) n -> p t n", p=128)[:, :, h * 128:(h + 1) * 128],
                          STG[s2][:, :].rearrange("p (t d) -> p t d", t=4), reads=[f"stg:{s2}"], writes=["out:vp"])
                if g > 0:
                    t = next_tp()
                    S.op("pe", lambda: nc.tensor.transpose(TP[t][0:64, 0:128], AR[:, cF + 512:cF + 576], ident[:, :]),
                         reads=ark(cF + 512, 64) + ["ident"], writes=[f"tp:{t}"])
                    S.op("act", lambda: nc.scalar.copy(Vs[0:64, h * 128:(h + 1) * 128], TP[t][0:64, 0:128]),
                         reads=[f"tp:{t}"], writes=[f"Vs:{h}"])
                    s2 = next_stg()
                    S.op("dve", lambda: nc.vector.tensor_copy(STG[s2][0:64, 0:128], TP[t][0:64, 0:128]),
                         reads=[f"tp:{t}"], writes=[f"stg:{s2}"])
                    S.dma("sp", f"g{s2}", v_s[stream * 64:(stream + 1) * 64, h * 128:(h + 1) * 128], STG[s2][0:64, 0:128],
                          reads=[f"stg:{s2}"], writes=["out:vs"])

            for fa, fb in ((k_a, k_b), (v_a, v_b)):
                fa(0)
                for h in range(NH):
                    if h + 1 < NH:
                        fa(h + 1)
                    fb(h)

        def attention(g, own, stream):
            prev = 1 - own
            cE = [0, 640]
            cH = [1280, 1920]
            cKS = [2560, 3072]
            cVS = [3584, 4096]
            cRI = [4608, 4608]

            def prologue_dma(h):
                b = h % 2
                S.dma("sp", f"hk{b}", AR[:, cH[b]:cH[b] + 640], bass.AP(ebias, h * 768, [[1, 128], [1, 640]]),
                      reads=["ebias"], writes=ark(cH[b], 640))
                S.op("pool", lambda: nc.gpsimd.memset(AR[0:64, cH[b]:cH[b] + 64], 0.0), writes=ark(cH[b], 64))
                S.op("pool", lambda: nc.gpsimd.memset(AR[64:128, cH[b] + 576:cH[b] + 640], 0.0), writes=ark(cH[b] + 576, 64))
                S.dma("sp", f"ks{b}", AR[:, cKS[b]:cKS[b] + 512].rearrange("p (t d) -> p t d", t=4),
                      ck[stream].rearrange("(t p) n -> p t n", p=128)[:, :, h * 128:(h + 1) * 128],
                      writes=ark(cKS[b], 512))
                S.dma("sp", f"vs{b}", AR[:, cVS[b]:cVS[b] + 512].rearrange("p (t d) -> p t d", t=4),
                      cv[stream].rearrange("(t p) n -> p t n", p=128)[:, :, h * 128:(h + 1) * 128],
                      writes=ark(cVS[b], 512))

            def prologue_pe(h):
                b = h % 2
                t = next_tp()
                for i in range(4):
                    S.op("pe", lambda: nc.tensor.transpose(TP[t][:, i * 128:(i + 1) * 128],
                                                           AR[:, cKS[b] + i * 128:cKS[b] + (i + 1) * 128], ident[:, :]),
                         reads=ark(cKS[b] + i * 128, 128) + ["ident"], writes=[f"tp:{t}"], signal=(i == 3))
                S.op("act", lambda: nc.scalar.copy(KTc[b][:, :], TP[t][:, :]), reads=[f"tp:{t}"], writes=[f"KTc:{b}"])
                S.op("pool", lambda: nc.gpsimd.tensor_copy(Vc[b][:, :, :], AR[:, cVS[b]:cVS[b] + 512].rearrange("p (t d) -> p t d", t=4)),
                     reads=ark(cVS[b], 512), writes=[f"Vc:{b}"])

            def stage_a(w):
                gi, h, kind, t_ = w
                b, eb = h % 2, gi % 2
                m = next_mm()
                qk = f"R:{h}:{t_}"
                if kind == "p":
                    q_ap = R[:, h, t_ * 128:(t_ + 1) * 128]
                    for mm_ in range(5):
                        c = t_ + mm_
                        half, cc = (prev, c) if c < 4 else (own, c - 4)
                        S.op("pe", lambda: nc.tensor.matmul(MM[m][:, mm_ * 128:(mm_ + 1) * 128],
                                                            KT[:, h, half * 512 + cc * 128: half * 512 + (cc + 1) * 128],
                                                            q_ap, start=True, stop=True),
                             reads=[f"KT:{half}:{h}", qk], writes=[f"mm:{m}:{mm_ // 4}"], signal=(mm_ == 4))
                    S.op("act", lambda: nc.scalar.activation(out=AR[:, cE[eb]:cE[eb] + 640], in_=MM[m][:, 0:640], func=AF.Exp),
                         reads=[f"mm:{m}:0", f"mm:{m}:1"], writes=ark(cE[eb], 640))
                    hrev = bass.AP(AR, cH[b] + 127, [[ARC, 128], [128, 5], [-1, 128]])
                    S.op("dve", lambda: nc.vector.tensor_tensor(
                        out=PT[eb][:, :].rearrange("p (m q) -> p m q", m=5),
                        in0=AR[:, cE[eb]:cE[eb] + 640].rearrange("p (m q) -> p m q", m=5), in1=hrev, op=ALU.mult),
                        reads=ark(cE[eb], 640) + ark(cH[b], 640), writes=[f"PT:{eb}"])
                else:
                    q_ap = R[:, h, 512:576]
                    for mm_ in range(4):
                        S.op("pe", lambda: nc.tensor.matmul(MM[m][:, mm_ * 64:(mm_ + 1) * 64],
                                                            KTc[b][:, mm_ * 128:(mm_ + 1) * 128], q_ap, start=True, stop=True),
                             reads=[f"KTc:{b}", qk], writes=[f"mm:{m}:0"], signal=False)
                    S.op("pe", lambda: nc.tensor.matmul(MM[m][0:64, 256:320], KTs[:, h, :], q_ap, start=True, stop=True),
                         reads=[f"KTs:{h}", qk], writes=[f"mm:{m}:0"])
                    S.op("act", lambda: nc.scalar.activation(out=AR[:, cE[eb]:cE[eb] + 256], in_=MM[m][:, 0:256], func=AF.Exp),
                         reads=[f"mm:{m}:0"], writes=ark(cE[eb], 256))
                    S.op("act", lambda: nc.scalar.activation(out=AR[0:64, cE[eb] + 256:cE[eb] + 320], in_=MM[m][0:64, 256:320], func=AF.Exp),
                         reads=[f"mm:{m}:0"], writes=ark(cE[eb] + 256, 64))
                    hrev = bass.AP(AR, cH[b] + 127, [[ARC, 128], [128, 4], [-1, 64]])
                    S.op("dve", lambda: nc.vector.tensor_tensor(
                        out=PT[eb][:, 0:256].rearrange("p (m q) -> p m q", m=4),
                        in0=AR[:, cE[eb]:cE[eb] + 256].rearrange("p (m q) -> p m q", m=4), in1=hrev, op=ALU.mult),
                        reads=ark(cE[eb], 256) + ark(cH[b], 640), writes=[f"PT:{eb}"])
                    hrev4 = bass.AP(AR, cH[b] + 512 + 127, [[ARC, 64], [-1, 64]])
                    S.op("dve", lambda: nc.vector.tensor_tensor(
                        out=PT[eb][0:64, 256:320], in0=AR[0:64, cE[eb] + 256:cE[eb] + 320], in1=hrev4, op=ALU.mult),
                        reads=ark(cE[eb] + 256, 64) + ark(cH[b], 640), writes=[f"PT:{eb}"])

            def stage_b(w):
                gi, h, kind, t_ = w
                b, eb = h % 2, gi % 2
                qk = f"R:{h}:{t_}"
                NQ = 128 if kind == "p" else 64
                q_ap = R[:, h, t_ * 128:(t_ + 1) * 128] if kind == "p" else R[:, h, 512:576]
                mo = next_mm()
                nchunk = 5
                for pass_ in range(2):
                    for mm_ in range(nchunk):
                        if kind == "p":
                            c = t_ + mm_
                            half, cc = (prev, c) if c < 4 else (own, c - 4)
                            halo = (g == 1 and c < 4)
                            if pass_ == 0:
                                lhs = V[:, half * 4 + cc, h * 128:(h + 1) * 128]
                                lk = [f"V:{half * 4 + cc}:{h}"]
                            else:
                                lhs = ones_hv[:, :] if halo else ones_b[:, :]
                                lk = ["ones_hv" if halo else "ones_b"]
                            rhs = PT[eb][:, mm_ * 128:(mm_ + 1) * 128]
                        else:
                            if mm_ < 4:
                                lhs = Vc[b][:, mm_, :] if pass_ == 0 else ones_b[:, :]
                                lk = [f"Vc:{b}"] if pass_ == 0 else ["ones_b"]
                                rhs = PT[eb][:, mm_ * 64:(mm_ + 1) * 64]
                            else:
                                lhs = Vs[0:64, h * 128:(h + 1) * 128] if pass_ == 0 else ones_b[0:64, :]
                                lk = [f"Vs:{h}"] if pass_ == 0 else ["ones_b"]
                                rhs = PT[eb][0:64, 256:320]
                        S.op("pe", lambda: nc.tensor.matmul(MM[mo][:, pass_ * 128:pass_ * 128 + NQ], lhs, rhs,
                                                            start=(mm_ == 0), stop=(mm_ == nchunk - 1)),
                             reads=lk + [f"PT:{eb}"], writes=[f"mm:{mo}:0"], signal=(pass_ == 1 and mm_ == nchunk - 1))
                ri = AR[:, cRI[eb]:cRI[eb] + NQ]
                S.op("dve", lambda: nc.vector.reciprocal(ri, MM[mo][:, 128:128 + NQ]),
                     reads=[f"mm:{mo}:0"], writes=ark(cRI[eb], NQ))
                S.op("dve", lambda: nc.vector.tensor_tensor(out=q_ap, in0=MM[mo][:, 0:NQ], in1=ri, op=ALU.mult),
                     reads=[f"mm:{mo}:0"] + ark(cRI[eb], NQ), writes=[qk])

            work = []
            for h in range(NH):
                for kind, t_ in [("p", 0), ("p", 1), ("p", 2), ("p", 3), ("s", 4)]:
                    work.append((len(work), h, kind, t_))
            prologue_dma(0)
            prologue_pe(0)
            prologue_dma(1)
            stage_a(work[0])
            for i, w in enumerate(work):
                if i + 1 < len(work):
                    nx = work[i + 1]
                    if nx[1] != w[1]:
                        prologue_pe(nx[1])
                        if nx[1] + 1 < NH:
                            prologue_dma(nx[1] + 1)
                    stage_a(nx)
                stage_b(w)

        def _emit_groups():
            for g in range(3):
                if g == 0:
                    segs = [Seg(2, 512)]
                    chsegs = [Seg(0, 257), Seg(257, 257)]
                    tiles = [(2 + 128 * i, 128) for i in range(4)]
                    own, stream = 0, 0
                    load_x(xp, 0, 2, 0)
                    for i in range(4):
                        load_x(xp, 2 + 128 * i, 128, 2 + 128 * i)
                else:
                    segs = [Seg(0, 288), Seg(288, 288)]
                    chsegs = segs
                    tiles = [(128 * i, 128) for i in range(4)] + [(512, 64)]
                    own, stream = (1, 0) if g == 1 else (0, 1)
                    for i in range(4):
                        load_x(xp, 514 + 512 * (g - 1) + 128 * i, 128, 128 * i)
                    load_x(xs, 64 * stream, 64, 512)

                _chk(f"g{g}_load")
                mixer(g, segs, chsegs, stream)
                _chk(f"g{g}_mixer")
                if g > 0:
                    store_state(sstate, "sstate", conv_s[stream])
                if g == 2:
                    store_state(carry, "carry", conv_p[:])
                for n in range(KC):
                    m = proj("w_out", n * 128, 0, KC, r_rhs, r_keys, segs)
                    resid_evac(m, n, segs)
                _chk(f"g{g}_wout")
                layer_norm(0, segs, tiles, True)
                _chk(f"g{g}_ln0")
                ffn(0, segs)
                _chk(f"g{g}_ffn0")
                layer_norm(1, segs, tiles, True)
                _chk(f"g{g}_ln1")
                kv_proj(g, segs, own, stream)
                _chk(f"g{g}_kv")
                if g == 0:
                    continue
                for h in range(NH):
                    m = proj("w_q", h * 128, 0, KC, x_rhs, x_keys, segs)
                    for si, sg in enumerate(segs):
                        S.op("act", lambda: nc.scalar.activation(out=R[:, h, sg.c0:sg.c0 + sg.n], in_=mmv(m, si, sg),
                                                                 func=AF.Identity, scale=QSCALE),
                             reads=[f"mm:{m}:{si}"], writes=rk(h, sg))
                _chk(f"g{g}_q")
                attention(g, own, stream)
                _chk(f"g{g}_attn")
                for n in range(KC):
                    m = proj("w_o", n * 128, 0, KC, r_rhs, r_keys, segs)
                    resid_evac(m, n, segs)
                layer_norm(2, segs, tiles, True)
                ffn(1, segs)
                layer_norm(3, segs, tiles, False)
                store_tiles([(y_p, 512 * (g - 1) + 128 * i, 128, 128 * i) for i in range(4)] + [(y_s, 64 * stream, 64, 512)])


        try:
            _chk("init")
            _emit_groups()
        except _Stop:
            pass
        for i_ in range(4):
            S.eng["sp"].wait_ge(S.sem[f"g{i_}"], 16 * S.n[f"g{i_}"])
        build_program.stats = (S.n_ins, S.n_wait, dict(S.n))
    return nc


def _layout_inputs(x_prompt, x_sample, state_conv, cache_k, cache_v, w_in_a, conv_w, w_out_a,
                   w_kv, w_q, w_o, rel_bias, ln_g, ln_b, w_gate_up, w_down):
    f = lambda a: np.ascontiguousarray(np.asarray(a, dtype=np.float32))
    xp_full = f(x_prompt)[0]
    xs_full = f(x_sample).reshape(16 * 64, D)
    sc = f(state_conv)[0]
    ckf = f(cache_k).reshape(16, 512, D)
    cvf = f(cache_v).reshape(16, 512, D)
    shared = {
        "w_in": f(w_in_a)[0], "conv_w": f(conv_w)[0].reshape(48, 128), "w_out": f(w_out_a)[0],
        "w_kv": f(w_kv), "w_q": f(w_q)[0], "w_o": f(w_o)[0], "rel_bias": f(rel_bias)[0],
        "ln_g": f(ln_g).reshape(64, 128), "ln_b": f(ln_b).reshape(64, 128),
        "w_gu0": f(w_gate_up)[0], "w_gu1": f(w_gate_up)[1], "w_dn0": f(w_down)[0], "w_dn1": f(w_down)[1],
    }
    padded = np.concatenate([np.zeros((514, D), np.float32), xp_full], axis=0)
    in_maps = []
    for i in range(NCORES):
        m = dict(shared)
        m["xp"] = np.ascontiguousarray(padded[1024 * i: 1024 * i + 1538])
        m["xs"] = np.ascontiguousarray(xs_full[128 * i: 128 * i + 128])
        m["sconv"] = np.ascontiguousarray(sc[2 * i: 2 * i + 2].reshape(2, 32, 128))
        m["ck"] = np.ascontiguousarray(ckf[2 * i: 2 * i + 2])
        m["cv"] = np.ascontiguousarray(cvf[2 * i: 2 * i + 2])
        m["hv"] = np.full((128, 1), 0.0 if i == 0 else 1.0, np.float32)
        in_maps.append(m)
    return in_maps


def kernel(**inputs):
    in_maps = _layout_inputs(**inputs)
    nc = build_program()
    res = run_bass_kernel_spmd(nc, in_maps, core_ids=list(range(NCORES)))
    r = res.results
    y_prompt = np.concatenate([r[i]["y_p"] for i in range(NCORES)], axis=0).reshape(1, 8192, D)
    y_sample = np.concatenate([r[i]["y_s"] for i in range(NCORES)], axis=0).reshape(16, 64, D)
    conv_prompt = r[NCORES - 1]["conv_p"].reshape(1, 1, 2, D)
    conv_sample = np.concatenate([r[i]["conv_s"].reshape(2, 2, D) for i in range(NCORES)], axis=0).reshape(1, 16, 2, D)
    k_prompt = r[NCORES - 1]["k_p"].reshape(1, 512, NH, 128)
    v_prompt = r[NCORES - 1]["v_p"].reshape(1, 512, NH, 128)
    k_sample = np.concatenate([r[i]["k_s"] for i in range(NCORES)], axis=0).reshape(16, 64, NH, 128)
    v_sample = np.concatenate([r[i]["v_s"] for i in range(NCORES)], axis=0).reshape(16, 64, NH, 128)
    outs = (y_prompt, y_sample, conv_prompt, conv_sample, k_prompt, v_prompt, k_sample, v_sample)
    return tuple(np.ascontiguousarray(o, dtype=np.float32) for o in outs)
```
